# Optimizing a Trainium2 kernel written in Bass

```python
import jax, jax.numpy as jnp
from jax import lax
import numpy as np

D_MODEL = 1024
BATCH = 32
SEQ = 2048
DEPTH = 1

N_MEM = 256
DIL_PATTERNS = ((128, 1), (512, 4), (2048, 16))
N_DIL_GROUPS = len(DIL_PATTERNS)
DIL_HEADS = 8
DIL_HEAD_DIM = D_MODEL // 16
DIL_WIDTH = DIL_HEADS * DIL_HEAD_DIM
RET_HEADS = 4
RET_QK_DIM = D_MODEL // 16
RET_V_DIM = 2 * RET_QK_DIM
RET_QK_WIDTH = RET_HEADS * RET_QK_DIM
RET_V_WIDTH = RET_HEADS * RET_V_DIM
RET_CHUNK = 128
MEM_HEADS = 4
MEM_HEAD_DIM = D_MODEL // 8
MEM_WIDTH = MEM_HEADS * MEM_HEAD_DIM
N_BRANCHES = 3
D_FF = 2816
CONV_WIDTH = 3
NORM_EPS = 1e-6
MASK_VALUE = -1e30
DIL_COLS = N_DIL_GROUPS * 3 * DIL_WIDTH
RET_COLS = 2 * RET_QK_WIDTH + 2 * RET_V_WIDTH
IN_COLS = DIL_COLS + RET_COLS + MEM_WIDTH + N_BRANCHES * D_MODEL

kernel_name = "hybrid_dilated_retention_memory_block"


def rms_norm(x, g):
    xf = x.astype(jnp.float32)
    y = xf * lax.rsqrt(jnp.mean(xf * xf, axis=-1, keepdims=True) + NORM_EPS)
    return (y * g.astype(jnp.float32)).astype(x.dtype)


def alibi_slopes(n_heads):
    exps = jnp.arange(1, n_heads + 1, dtype=jnp.float32) * (8.0 / n_heads)
    return jnp.exp2(-exps)


def dilated_window_attention(q, k, v, dilation, half, slopes):
    b, s, h, dh = q.shape
    r = dilation
    n_sub = s // r
    nb = -(-n_sub // half)
    lp = nb * half

    def residue_split(t):
        t = t.reshape(b, n_sub, r, h, dh).transpose(0, 2, 3, 1, 4)
        return jnp.pad(t, ((0, 0), (0, 0), (0, 0), (0, lp - n_sub), (0, 0)))

    def band(t):
        tp = jnp.pad(t, ((0, 0), (0, 0), (0, 0), (half, half), (0, 0))).reshape(b, r, h, nb + 2, half, dh)
        return jnp.concatenate([tp[:, :, :, :-2], tp[:, :, :, 1:-1], tp[:, :, :, 2:]], axis=4)

    qb = residue_split(q).reshape(b, r, h, nb, half, dh)
    kb = band(residue_split(k))
    vb = band(residue_split(v))
    scores = jnp.einsum('brhnqd,brhnkd->brhnqk', qb, kb) * (dh ** -0.5)
    q_idx = jnp.arange(nb)[:, None] * half + jnp.arange(half)[None, :]
    k_idx = jnp.arange(nb)[:, None] * half - half + jnp.arange(3 * half)[None, :]
    rel = k_idx[:, None, :] - q_idx[:, :, None]
    valid = (jnp.abs(rel) <= half) & (k_idx[:, None, :] >= 0) & (k_idx[:, None, :] < n_sub)
    dist = (jnp.abs(rel) * r).astype(jnp.float32)
    scores = scores - slopes[:, None, None, None] * dist[None]
    scores = jnp.where(valid, scores, MASK_VALUE)
    m = jnp.max(scores, axis=-1, keepdims=True)
    p = jnp.exp(scores - m)
    denom = jnp.sum(p, axis=-1)
    o = jnp.einsum('brhnqk,brhnkd->brhnqd', p, vb) / denom[..., None]
    lse = m[..., 0] + jnp.log(denom)
    o = o.reshape(b, r, h, lp, dh)[:, :, :, :n_sub].transpose(0, 3, 1, 2, 4).reshape(b, s, h, dh)
    lse = lse.reshape(b, r, h, lp)[:, :, :, :n_sub].transpose(0, 3, 1, 2).reshape(b, s, h)
    return o, lse


def dilated_mixture_attention(p, q_norm_g, k_norm_g, slopes):
    b, s, _ = p.shape
    p = p.reshape(b, s, N_DIL_GROUPS, 3, DIL_HEADS, DIL_HEAD_DIM)
    outs, lses = [], []
    for g, (window, dilation) in enumerate(DIL_PATTERNS):
        q = rms_norm(p[:, :, g, 0], q_norm_g[g]).astype(jnp.float32)
        k = rms_norm(p[:, :, g, 1], k_norm_g[g]).astype(jnp.float32)
        v = p[:, :, g, 2].astype(jnp.float32)
        o, lse = dilated_window_attention(q, k, v, dilation, window // (2 * dilation), slopes)
        outs.append(o)
        lses.append(lse)
    weights = jax.nn.softmax(jnp.stack(lses), axis=0)
    o = jnp.einsum('gbsh,gbshd->bshd', weights, jnp.stack(outs))
    return o.reshape(b, s, DIL_WIDTH).astype(p.dtype)


def chunkwise_retention(q, k, v, log_gamma, strict):
    b, h, s, dk = q.shape
    dv = v.shape[-1]
    c = RET_CHUNK
    n = s // c
    qc = q.reshape(b, h, n, c, dk)
    kc = k.reshape(b, h, n, c, dk)
    vc = v.reshape(b, h, n, c, dv)
    idx = jnp.arange(c)
    diff = idx[:, None] - idx[None, :]
    mask = (diff > 0) if strict else (diff >= 0)
    decay = jnp.where(mask[None], jnp.exp(log_gamma[:, None, None] * jnp.maximum(diff, 0).astype(jnp.float32)[None]), 0.0)
    inner = jnp.einsum('bhnid,bhnjd->bhnij', qc, kc) * decay[:, None]
    y_inner = jnp.einsum('bhnij,bhnjv->bhniv', inner, vc)
    zeta = jnp.exp(log_gamma[:, None] * (c - 1 - idx).astype(jnp.float32))
    xi = jnp.exp(log_gamma[:, None] * (idx + 1).astype(jnp.float32))
    chunk_decay = jnp.exp(log_gamma * c)[None, :, None, None]
    u = jnp.einsum('bhnjd,bhnjv->nbhdv', kc * zeta[:, None, :, None], vc)

    def step(state, u_i):
        return state * chunk_decay + u_i, state

    _, prev = lax.scan(step, jnp.zeros((b, h, dk, dv), jnp.float32), u)
    y_cross = jnp.einsum('bhnid,nbhdv->bhniv', qc * xi[:, None, :, None], prev)
    return (y_inner + y_cross).reshape(b, h, s, dv)


def bidirectional_retention(p, decay_logit, gn_g):
    b, s, _ = p.shape
    q, k, v, gate = jnp.split(p, [RET_QK_WIDTH, 2 * RET_QK_WIDTH, 2 * RET_QK_WIDTH + RET_V_WIDTH], axis=-1)

    def heads(t, d):
        return t.reshape(b, s, RET_HEADS, d).transpose(0, 2, 1, 3).astype(jnp.float32)

    q = heads(q, RET_QK_DIM)
    k = heads(k, RET_QK_DIM) * (RET_QK_DIM ** -0.5)
    v = heads(v, RET_V_DIM)
    log_gamma = jax.nn.log_sigmoid(decay_logit.astype(jnp.float32))
    fwd = chunkwise_retention(q, k, v, log_gamma[0], strict=False)
    bwd = jnp.flip(chunkwise_retention(jnp.flip(q, 2), jnp.flip(k, 2), jnp.flip(v, 2), log_gamma[1], strict=True), 2)
    y = fwd + bwd
    mu = jnp.mean(y, axis=-1, keepdims=True)
    var = jnp.mean(jnp.square(y - mu), axis=-1, keepdims=True)
    y = (y - mu) * lax.rsqrt(var + NORM_EPS)
    y = y.transpose(0, 2, 1, 3).reshape(b, s, RET_V_WIDTH) * gn_g.astype(jnp.float32)
    return (jax.nn.silu(gate.astype(jnp.float32)) * y).astype(p.dtype)


def memory_cross_attention(q_p, mem, mem_norm_g, w_mem_kv, q_norm_g, k_norm_g):
    b, s, _ = q_p.shape
    m = mem.shape[1]
    q = rms_norm(q_p.reshape(b, s, MEM_HEADS, MEM_HEAD_DIM), q_norm_g).astype(jnp.float32)
    kv = (rms_norm(mem, mem_norm_g) @ w_mem_kv).reshape(b, m, 2, MEM_HEADS, MEM_HEAD_DIM)
    k = rms_norm(kv[:, :, 0], k_norm_g).astype(jnp.float32)
    v = kv[:, :, 1].astype(jnp.float32)
    scores = jnp.einsum('bshd,bmhd->bhsm', q, k) * (MEM_HEAD_DIM ** -0.5)
    attn = jax.nn.softmax(scores, axis=-1)
    o = jnp.einsum('bhsm,bmhd->bshd', attn, v)
    return o.reshape(b, s, MEM_WIDTH).astype(q_p.dtype)


def conv_glu_ffn(h, w_ffn_in, conv_w, conv_b, w_ffn_out):
    u, gate = jnp.split(h @ w_ffn_in, 2, axis=-1)
    s = u.shape[1]
    pad = CONV_WIDTH // 2
    up = jnp.pad(u, ((0, 0), (pad, pad), (0, 0)))
    c = conv_b
    for i in range(CONV_WIDTH):
        c = c + up[:, i:i + s] * conv_w[i]
    y = jax.nn.gelu(c, approximate=False) * gate
    return y @ w_ffn_out


def setup_inputs(seed: int = 0) -> dict:
    key = jax.random.key(seed)
    ks = jax.random.split(key, 24)
    f32 = jnp.float32

    def normal(k, shape, scale):
        return jax.random.normal(k, shape, f32) * scale

    def gain(k, shape):
        return 1.0 + 0.02 * jax.random.normal(k, shape, f32)

    base_logit = jnp.log(jnp.exp2(5.0 + jnp.arange(RET_HEADS, dtype=f32)) - 1.0)
    ret_decay_logit = base_logit[None, None, :] + 0.1 * jax.random.normal(ks[6], (DEPTH, 2, RET_HEADS), f32)
    return {
        'x': normal(ks[0], (BATCH, SEQ, D_MODEL), 1.0),
        'mem': normal(ks[1], (BATCH, N_MEM, D_MODEL), 1.0),
        'norm1_g': gain(ks[2], (DEPTH, D_MODEL)),
        'w_in': normal(ks[3], (DEPTH, D_MODEL, IN_COLS), D_MODEL ** -0.5),
        'dil_q_norm_g': gain(ks[4], (DEPTH, N_DIL_GROUPS, DIL_HEAD_DIM)),
        'dil_k_norm_g': gain(ks[5], (DEPTH, N_DIL_GROUPS, DIL_HEAD_DIM)),
        'ret_decay_logit': ret_decay_logit,
        'ret_gn_g': gain(ks[7], (DEPTH, RET_V_WIDTH)),
        'mem_norm_g': gain(ks[8], (DEPTH, D_MODEL)),
        'w_mem_kv': normal(ks[9], (DEPTH, D_MODEL, 2 * MEM_WIDTH), D_MODEL ** -0.5),
        'mem_q_norm_g': gain(ks[10], (DEPTH, MEM_HEAD_DIM)),
        'mem_k_norm_g': gain(ks[11], (DEPTH, MEM_HEAD_DIM)),
        'w_branch_dil': normal(ks[12], (DEPTH, DIL_WIDTH, D_MODEL), DIL_WIDTH ** -0.5),
        'w_branch_ret': normal(ks[13], (DEPTH, RET_V_WIDTH, D_MODEL), RET_V_WIDTH ** -0.5),
        'w_branch_mem': normal(ks[14], (DEPTH, MEM_WIDTH, D_MODEL), MEM_WIDTH ** -0.5),
        'w_out': normal(ks[15], (DEPTH, D_MODEL, D_MODEL), D_MODEL ** -0.5),
        'norm2_g': gain(ks[16], (DEPTH, D_MODEL)),
        'w_ffn_in': normal(ks[17], (DEPTH, D_MODEL, 2 * D_FF), D_MODEL ** -0.5),
        'ffn_conv_w': normal(ks[18], (DEPTH, CONV_WIDTH, D_FF), CONV_WIDTH ** -0.5),
        'ffn_conv_b': normal(ks[19], (DEPTH, D_FF), 0.01),
        'w_ffn_out': normal(ks[20], (DEPTH, D_FF, D_MODEL), D_FF ** -0.5),
    }


def reference(x, mem, norm1_g, w_in, dil_q_norm_g, dil_k_norm_g, ret_decay_logit, ret_gn_g,
              mem_norm_g, w_mem_kv, mem_q_norm_g, mem_k_norm_g, w_branch_dil, w_branch_ret,
              w_branch_mem, w_out, norm2_g, w_ffn_in, ffn_conv_w, ffn_conv_b, w_ffn_out):
    b, s, d = x.shape
    dt = x.dtype
    slopes = alibi_slopes(DIL_HEADS)
    split_at = [DIL_COLS, DIL_COLS + RET_COLS, DIL_COLS + RET_COLS + MEM_WIDTH]
    for l in range(DEPTH):
        h = rms_norm(x, norm1_g[l])
        proj = h @ w_in[l]
        dil_p, ret_p, memq_p, gate_p = jnp.split(proj, split_at, axis=-1)
        y_dil = dilated_mixture_attention(dil_p, dil_q_norm_g[l], dil_k_norm_g[l], slopes)
        y_ret = bidirectional_retention(ret_p, ret_decay_logit[l], ret_gn_g[l])
        y_mem = memory_cross_attention(memq_p, mem, mem_norm_g[l], w_mem_kv[l], mem_q_norm_g[l], mem_k_norm_g[l])
        gates = jax.nn.sigmoid(gate_p.astype(jnp.float32)).reshape(b, s, N_BRANCHES, d)
        merged = (gates[:, :, 0] * (y_dil @ w_branch_dil[l]).astype(jnp.float32)
                  + gates[:, :, 1] * (y_ret @ w_branch_ret[l]).astype(jnp.float32)
                  + gates[:, :, 2] * (y_mem @ w_branch_mem[l]).astype(jnp.float32))
        x = x + merged.astype(dt) @ w_out[l]
        h2 = rms_norm(x, norm2_g[l])
        x = x + conv_glu_ffn(h2, w_ffn_in[l], ffn_conv_w[l], ffn_conv_b[l], w_ffn_out[l]).astype(dt)
    return x
```

```python
import types
import numpy as np
from contextlib import ExitStack
import concourse.bass as bass
import concourse.mybir as mybir
from concourse.bass_utils import run_bass_kernel_spmd

F32 = mybir.dt.float32
BF16 = mybir.dt.bfloat16
AF = mybir.ActivationFunctionType
ALU = mybir.AluOpType

S = 2048
D = 1024
NCORES = 8
NSEQ = 4
NMEM = 256
DFF = 2816
NJ = 22
EPS = 1e-6
DIL = ((128, 1), (512, 4), (2048, 16))
C_DIL = 4608
C_RET = 4608
C_MEMQ = 6144
C_GATE = 6656

P_N1G, P_MNG, P_N2G = 0, 8, 16
P_DQG, P_DKG = 24, 27
P_MQG, P_MKG = 30, 31
P_GNG = 32
P_CW = 36
P_CB = 102
P_LG = 124
P_LGP = 132
NP = 136

K_ID, K_BD, K_O128, K_ONE = 0, 128, 256, 384
K_DIST = 512
K_DPOS, K_DNEG, K_MF, K_MB = 1024, 1152, 1280, 1408
K_ZF, K_ZB = 1536, 1537
K_XF, K_XB = 1538, 1666
NCF = 1794


def _freeze(fn):
    if fn is None or fn.__closure__ is None:
        return fn
    cells = []
    for c in fn.__closure__:
        try:
            cells.append(types.CellType(c.cell_contents))
        except ValueError:
            cells.append(c)
    return types.FunctionType(fn.__code__, fn.__globals__, fn.__name__, fn.__defaults__, tuple(cells))


class Prog:
    ENG = ("tensor", "vector", "scalar", "gpsimd", "sync")

    def __init__(self, nc, ctx):
        self.nc = nc
        self.ctx = ctx
        self.ops = []
        self.res = {}
        self.last = {e: None for e in self.ENG}
        self.bar = {e: set() for e in self.ENG}
        self.out_dmas = []
        self.last_dma = {}

    def _r(self, key):
        if key not in self.res:
            self.res[key] = {"w": None, "r": []}
        return self.res[key]

    def op(self, eng, fn, reads=(), writes=(), dma=None, is_out=False):
        idx = len(self.ops)
        deps = {}

        def add(d, kind):
            if d not in deps or kind < deps[d]:
                deps[d] = kind
        psum_reads = [k for k in reads if isinstance(k, str) and k.startswith("ps")]
        for k in reads:
            r = self._r(k)
            if r["w"] is not None:
                add(r["w"], 0)
        for k in writes:
            r = self._r(k)
            if r["w"] is not None:
                add(r["w"], 1)
            for t in r["r"]:
                add(t, 1)
        for k in psum_reads:
            r = self._r(k)
            for t in r["r"]:
                add(t, 2)
        for d in self.bar[eng]:
            add(d, 0)
        self.bar[eng] = set()
        deps.pop(idx, None)
        self.ops.append({"eng": eng, "fn": _freeze(fn), "deps": deps, "dma": dma})
        for k in reads:
            self._r(k)["r"].append(idx)
        for k in writes:
            r = self._r(k)
            r["w"] = idx
            r["r"] = []
        self.last[eng] = idx
        if dma is not None:
            self.last_dma[dma] = idx
        if is_out:
            self.out_dmas.append(idx)
        return idx

    def barrier(self, engines=("tensor", "vector", "scalar", "sync")):
        lasts = set(self.last[e] for e in engines if self.last[e] is not None)
        lasts |= set(self.last_dma.values())
        for e in engines:
            self.bar[e] |= lasts

    def emit(self):
        nc = self.nc
        ops = self.ops
        idx = len(ops)
        ops.append({"eng": "sync", "fn": None, "deps": {d: 0 for d in self.out_dmas}, "dma": None})
        needed = [False] * len(ops)
        for i, o in enumerate(ops):
            keep = set()
            for d, kind in o["deps"].items():
                po = ops[d]
                if po["dma"] is None and o["dma"] is None and po["eng"] == o["eng"]:
                    if o["eng"] == "tensor" or kind == 2:
                        continue
                keep.add(d)
            o["deps"] = keep
            for d in keep:
                needed[d] = True
        eng_sem = {}
        eng_cnt = {e: 0 for e in self.ENG}
        for e in self.ENG:
            eng_sem[e] = self.ctx.enter_context(nc.semaphore("sem_" + e))
        dma_sem = {}
        dma_cnt = {}
        for i, o in enumerate(ops):
            if o["dma"] is not None:
                k = o["dma"]
                if k not in dma_sem:
                    dma_sem[k] = self.ctx.enter_context(nc.semaphore("dsem_%d" % len(dma_sem)))
                    dma_cnt[k] = 0
                dma_cnt[k] += 16
                o["tok"] = (dma_sem[k], dma_cnt[k])
                o["inc"] = (dma_sem[k], 16)
            elif needed[i]:
                eng_cnt[o["eng"]] += 1
                o["tok"] = (eng_sem[o["eng"]], eng_cnt[o["eng"]])
                o["inc"] = (eng_sem[o["eng"]], 1)
            else:
                o["tok"] = None
                o["inc"] = None
        self.stats = {"ops": len(ops), "eng_cnt": dict(eng_cnt), "dma_sems": len(dma_sem),
                      "per_eng": {e: sum(1 for o in ops if o["eng"] == e) for e in self.ENG}}
        by_eng = {e: [] for e in self.ENG}
        for i, o in enumerate(ops):
            by_eng[o["eng"]].append(i)

        def run(e, name):
            waited = {}
            for i in by_eng[name]:
                o = ops[i]
                w = {}
                for d in o["deps"]:
                    sem, val = ops[d]["tok"]
                    key = id(sem)
                    if key not in w or w[key][1] < val:
                        w[key] = (sem, val)
                for key, (sem, val) in w.items():
                    if waited.get(key, 0) >= val:
                        continue
                    waited[key] = val
                    e.wait_ge(sem, val)
                if o["fn"] is None:
                    continue
                ins = o["fn"](e)
                if o["inc"] is not None:
                    ins.then_inc(o["inc"][0], o["inc"][1])

        with nc.Block() as block:
            @block.tensor
            def _(e):
                run(e, "tensor")

            @block.vector
            def _(e):
                run(e, "vector")

            @block.scalar
            def _(e):
                run(e, "scalar")

            @block.gpsimd
            def _(e):
                run(e, "gpsimd")

            @block.sync
            def _(e):
                run(e, "sync")


TUNE = {"ilv": "front", "lag": 2, "aov": False}


def build(nseq=NSEQ, upto="all", dbg=()):
    nc = bass.Bass("TRN2", target_bir_lowering=False)

    def din(name, shape):
        return nc.dram_tensor(name, list(shape), F32, kind="ExternalInput").ap()

    x_d = din("x", [nseq, S, D])
    mem_d = din("mem", [nseq, NMEM, D])
    w_in = din("w_in", [D, 9728])
    w_kv = din("w_mem_kv", [D, 1024])
    w_bd = din("w_branch_dil", [512, D])
    w_br = din("w_branch_ret", [512, D])
    w_bm = din("w_branch_mem", [512, D])
    w_out = din("w_out", [D, D])
    w_f1 = din("w_ffn_in", [D, 2 * DFF])
    w_f2 = din("w_ffn_out", [DFF, D])
    par_d = din("params", [128, NP])
    cst_d = din("consts", [128, NCF])
    out_d = nc.dram_tensor("out", [nseq, S, D], F32, kind="ExternalOutput").ap()
    dbg_d = {}
    for name, shape in dbg:
        dbg_d[name] = nc.dram_tensor("dbg_" + name, list(shape), F32, kind="ExternalOutput").ap()

    with ExitStack() as ctx:
        P = Prog(nc, ctx)

        def sb(name, shape, dt):
            return ctx.enter_context(nc.sbuf_tensor(name, list(shape), dt))

        par = sb("par", [128, NP], F32)
        cf = sb("cf", [128, 1024 - 512 + 0], F32)
        cb = sb("cb", [128, 512], BF16)
        rc = sb("rc", [128, 4 * 128 + 256 + 256 + 2 * 256 + 8], F32)
        lg = sb("lg", [128, 16], F32)
        hT = sb("hT", [128, 8, S], BF16)
        wbuf = [sb("wbuf%d" % i, [128, 8 * 512], BF16) for i in range(3)]
        arena = sb("arena", [128, 61440], BF16)
        xt = [sb("xt%d" % i, [128, D], F32) for i in range(2)]
        xs = sb("xs", [128, D], BF16)
        st = sb("st", [128, 8], F32)
        t32 = [sb("t32_%d" % i, [128, 512], F32) for i in range(3)]
        tball = sb("tball", [128, 2048], BF16)
        tb = [tball[:, i * 512:(i + 1) * 512] for i in range(4)]
        stt = sb("stt", [128, 512], F32)
        ps = [ctx.enter_context(nc.psum_tensor("ps%d" % i, [128, 512], F32)) for i in range(8)]

        def av(off, n):
            return arena[:, off:off + n]

        def avf(off, n):
            return arena[:, off:off + 2 * n].bitcast(F32)

        Y0, E0, S0 = 0, 24576, 40960
        yretT = av(Y0, 8192).rearrange("p (c t) -> p c t", c=4)
        ymemT = av(Y0 + 8192, 8192).rearrange("p (c t) -> p c t", c=4)
        ydilT = av(Y0 + 16384, 8192).rearrange("p (c t) -> p c t", c=4)
        rv = av(Y0 + 8192, 8192).rearrange("p (n c) -> p n c", n=16)
        kzf = av(Y0 + 16384, 4096).rearrange("p (n c) -> p n c", n=16)
        kzb = av(Y0 + 20480, 4096).rearrange("p (n c) -> p n c", n=16)
        rq = av(E0, 4096).rearrange("p (c t) -> p c t", c=2)
        rk = av(E0 + 4096, 4096).rearrange("p (c t) -> p c t", c=2)
        rgate = av(E0 + 8192, 8192).rearrange("p (c t) -> p c t", c=4)
        SfA = av(S0, 4096).rearrange("p (c n v) -> p c n v", c=2, n=16)
        SbA = av(S0 + 4096, 4096).rearrange("p (c n v) -> p c n v", c=2, n=16)
        S32 = avf(S0 + 8192, 2048).rearrange("p (n v) -> p n v", n=16)
        qxf = av(S0 + 12288, 4096).rearrange("p (c t) -> p c t", c=2)
        qxb = av(S0 + 16384, 4096).rearrange("p (c t) -> p c t", c=2)
        memnT = av(S0, 2048).rearrange("p (k t) -> p k t", k=8)
        mk = av(S0 + 2048, 1024).rearrange("p (h t) -> p h t", h=4)
        mv = av(S0 + 3072, 1024).rearrange("p (m c) -> p m c", m=2)
        dqs = [av(E0, 4096), av(S0 + 2048, 4096)]
        dks = [av(E0 + 4096, 2048), av(S0 + 6144, 2048)]
        dvs = [av(S0 + 8192, 3072).rearrange("p (n c) -> p n c", n=16), av(S0 + 11264, 3072).rearrange("p (n c) -> p n c", n=16)]
        accA = avf(E0 + 6144, 2048)
        accB = avf(E0 + 10240, 2048)
        tmpR = avf(E0 + 14336, 1024)
        Ets = [[av(S0 + (st_ * 2 + hh) * 512, 512) for hh in range(2)] for st_ in range(2)]
        praws = [av(S0 + 14336 + i * 512, 512) for i in range(4)]
        pTs = [av(S0 + 16384 + i * 512, 512) for i in range(4)]
        mergedT = av(E0, 16384).rearrange("p (c t) -> p c t", c=8)
        wbr = [av(S0 + i * 4096, 4096).rearrange("p (k n) -> p k n", k=4) for i in range(3)]
        yT = av(0, 45056).rearrange("p (j t) -> p j t", j=NJ)
        ubuf = avf(45056, 2050)
        cbuf = avf(45056 + 4100, 2048)

        ident = cb[:, 0:128]
        bd64 = cb[:, 128:256]
        o128 = cb[:, 256:384]
        ones = cb[:, 384:512]
        dist4 = cf[:, 0:512]
        dcomb = rc[:, 0:512].rearrange("p (h i) -> p h i", h=4)
        zf = rc[:, 512:768]
        zb = rc[:, 768:1024]
        xf = rc[:, 1024:1280].rearrange("p (c i) -> p c i", c=2)
        xb = rc[:, 1280:1536].rearrange("p (c i) -> p c i", c=2)
        gC = rc[:, 1536:1540]

        cnt = {"bank": 0, "w": 0, "t32": 0, "tb": 0, "xt": 0, "nt": 0}

        def bank():
            i = cnt["bank"] % 8
            cnt["bank"] += 1
            return ps[i], "ps%d" % i

        def nxt(kind, n):
            i = cnt[kind] % n
            cnt[kind] += 1
            return i

        def wload(src_ap, ncols, view=None):
            s = nxt("w", 3)
            dst = wbuf[s][:, 0:8 * ncols].rearrange("p (k n) -> p k n", k=8)
            pairs = src_ap if isinstance(src_ap, list) else [(src_ap, view)]
            for sa, vw in pairs:
                P.op("gpsimd", lambda e, sa=sa, vw=vw: e.dma_start(out=dst if vw is None else vw(dst), in_=sa),
                     writes=["wbuf%d" % s], dma="wbuf%d" % s)
            return dst, "wbuf%d" % s

        def rows(ap):
            return ap.rearrange("(k p) n -> p k n", p=128)

        def dump(name, src_ap, key, dram_view=None):
            if name in dbg_d:
                dst = dbg_d[name] if dram_view is None else dram_view(dbg_d[name])
                P.op("gpsimd", lambda e: e.dma_start(out=dst, in_=src_ap), reads=[key], dma="dbg_" + name, is_out=True)

        P.op("sync", lambda e: e.dma_start(out=par[:], in_=par_d), writes=["par"], dma="par")
        P.op("sync", lambda e: e.dma_start(out=cf[:], in_=cst_d[:, K_DIST:K_DIST + 512]), writes=["cf"], dma="cf")
        P.op("gpsimd", lambda e: e.dma_start(out=cb[:], in_=cst_d[:, 0:512]), writes=["cb"], dma="cb")
        P.op("sync", lambda e: e.dma_start(out=rc[:, 0:512], in_=cst_d[:, K_DPOS:K_DPOS + 512]), writes=["rc"], dma="rc")
        P.op("scalar", lambda e: e.activation(out=lg[:, 0:12], in_=par[:, P_LG:P_LG + 12], func=AF.Exp, scale=-1.0),
             reads=["par"], writes=["lg"])
        P.op("scalar", lambda e: e.activation(out=lg[:, 0:12], in_=lg[:, 0:12], func=AF.Ln, bias=1.0, scale=1.0),
             reads=["lg"], writes=["lg"])
        P.op("scalar", lambda e: e.mul(out=lg[:, 0:12], in_=lg[:, 0:12], mul=-1.0), reads=["lg"], writes=["lg"])
        P.op("vector", lambda e: e.tensor_copy(out=t32[0][:], in_=rc[:, 0:512]), reads=["rc"], writes=["t32_0"])
        for h in range(4):
            P.op("scalar", lambda e, h=h: e.activation(out=t32[1][:, 0:128], in_=t32[0][:, 0:128], func=AF.Exp,
                                                        scale=lg[:, h:h + 1]), reads=["t32_0", "lg"], writes=["t32_1"])
            P.op("scalar", lambda e, h=h: e.activation(out=t32[1][:, 128:256], in_=t32[0][:, 128:256], func=AF.Exp,
                                                        scale=lg[:, 4 + h:5 + h]), reads=["t32_0", "lg"], writes=["t32_1"])
            P.op("vector", lambda e: e.tensor_tensor(out=t32[1][:, 0:256], in0=t32[1][:, 0:256], in1=t32[0][:, 256:512],
                                                     op=ALU.mult), reads=["t32_1", "t32_0"], writes=["t32_1"])
            P.op("vector", lambda e, h=h: e.tensor_tensor(out=dcomb[:, h, :], in0=t32[1][:, 0:128], in1=t32[1][:, 128:256],
                                                           op=ALU.add), reads=["t32_1"], writes=["rc"])
            P.op("vector", lambda e, h=h: e.tensor_scalar(out=dcomb[:, h, :], in0=dcomb[:, h, :], scalar1=0.125, scalar2=None,
                                                           op0=ALU.mult), reads=["rc"], writes=["rc"])
        P.op("sync", lambda e: e.dma_start(out=t32[2][:, 0:2], in_=cst_d[:, K_ZF:K_ZF + 2]), writes=["t32_2"], dma="t32_2")
        for h in range(4):
            for dr, dst_t in ((0, zf), (1, zb)):
                P.op("scalar", lambda e, h=h, dr=dr, dst_t=dst_t: e.activation(
                    out=dst_t[:, h * 64:(h + 1) * 64], in_=t32[2][:, dr:dr + 1].to_broadcast([128, 64]), func=AF.Exp,
                    scale=lg[:, dr * 4 + h:dr * 4 + h + 1]), reads=["t32_2", "lg"], writes=["rc"])
        P.op("vector", lambda e: e.tensor_scalar(out=rc[:, 512:1024], in0=rc[:, 512:1024], scalar1=0.125, scalar2=None,
                                                 op0=ALU.mult), reads=["rc"], writes=["rc"])
        P.op("sync", lambda e: e.dma_start(out=t32[2][:, 128:384], in_=cst_d[:, K_XF:K_XF + 256]), writes=["t32_2"], dma="t32_2")
        for ch in range(2):
            P.op("scalar", lambda e, ch=ch: e.activation(out=xf[:, ch, :], in_=t32[2][:, 128:256], func=AF.Exp,
                                                          scale=lg[:, 8 + ch * 2:9 + ch * 2]), reads=["t32_2", "lg"], writes=["rc"])
            P.op("scalar", lambda e, ch=ch: e.activation(out=xb[:, ch, :], in_=t32[2][:, 256:384], func=AF.Exp,
                                                          scale=lg[:, 9 + ch * 2:10 + ch * 2]), reads=["t32_2", "lg"], writes=["rc"])
        P.op("scalar", lambda e: e.activation(out=gC, in_=lg[:, 8:12], func=AF.Exp, scale=128.0), reads=["lg"], writes=["rc"])

        def rstd_from_ss(ss_ap, n, scale, key):
            P.op("scalar", lambda e: e.activation(out=ss_ap, in_=ss_ap, func=AF.Ln, scale=scale, bias=eps_ap),
                 reads=[key, "st"], writes=[key])
            P.op("scalar", lambda e: e.activation(out=ss_ap, in_=ss_ap, func=AF.Exp, scale=-0.5), reads=[key], writes=[key])

        eps_ap = st[:, 7:8]
        P.op("vector", lambda e: e.memset(st[:, 7:8], EPS), writes=["st"])

        def norm_a(src, src_key):
            si = nxt("nt", 2)
            if si == 0:
                xb_, xkeys = xs[:], ["xs"]
            else:
                xb_, xkeys = tball[:, 0:1024], ["tb_0", "tb_1"]
            sc, sk = st[:, si:si + 1], "st%d" % si
            P.op("scalar", lambda e: e.activation(out=xb_, in_=src, func=AF.Square, accum_out=sc),
                 reads=[src_key], writes=xkeys + [sk])
            P.op("scalar", lambda e: e.activation(out=sc, in_=sc, func=AF.Ln, scale=1.0 / D, bias=eps_ap),
                 reads=[sk, "st"], writes=[sk])
            P.op("scalar", lambda e: e.activation(out=sc, in_=sc, func=AF.Exp, scale=-0.5), reads=[sk], writes=[sk])
            P.op("vector", lambda e: e.tensor_scalar(out=xb_, in0=src, scalar1=sc, scalar2=None, op0=ALU.mult),
                 reads=[src_key, sk], writes=xkeys)
            return xb_, xkeys

        def norm_b(xb_, xkeys, dstT, dst_key, tcol, gcol):
            bk, bkey = bank()
            pstv = bk[:].bitcast(BF16)
            for kc in range(8):
                P.op("tensor", lambda e, kc=kc: e.transpose(out=pstv[:, kc * 128:(kc + 1) * 128],
                                                             in_=xb_[:, kc * 128:(kc + 1) * 128], identity=ident),
                     reads=xkeys + ["cb"], writes=[bkey])
            P.op("vector", lambda e: e.tensor_tensor(
                out=dstT[:, :, tcol:tcol + 128], in0=pstv.rearrange("p (k i) -> p k i", k=8),
                in1=par[:, gcol:gcol + 8].unsqueeze(2).to_broadcast([128, 8, 128]), op=ALU.mult),
                reads=[bkey, "par"], writes=[dst_key])

        def norm_transpose(src, src_key, dstT, dst_key, tcol, gcol):
            xb_, xkeys = norm_a(src, src_key)
            norm_b(xb_, xkeys, dstT, dst_key, tcol, gcol)

        def proj_fm(w, wkey, coff, tt, rhsT=hT, rkey="hT", n=512, toff=None):
            bk, bkey = bank()
            t0 = tt * 512 if toff is None else toff
            for kc in range(8):
                P.op("tensor", lambda e, kc=kc: e.matmul(bk[:, 0:n], lhsT=w[:, kc, coff:coff + 128],
                                                          rhs=rhsT[:, kc, t0:t0 + n], start=(kc == 0), stop=(kc == 7)),
                     reads=[wkey, rkey], writes=[bkey])
            return bk, bkey

        def headnorm(bk, bkey, n, onesm, gcol, dst_fn, dkey):
            i1 = nxt("tb", 4)
            sq, sqk = tb[i1], "tb_%d" % i1
            P.op("scalar", lambda e: e.activation(out=sq[:, 0:n], in_=bk[:, 0:n], func=AF.Square), reads=[bkey], writes=[sqk])
            b2, b2k = bank()
            P.op("tensor", lambda e: e.matmul(b2[:, 0:n], lhsT=onesm, rhs=sq[:, 0:n], start=True, stop=True),
                 reads=[sqk, "cb"], writes=[b2k])
            i2 = nxt("t32", 3)
            r, rk_ = t32[i2], "t32_%d" % i2
            P.op("scalar", lambda e: e.activation(out=r[:, 0:n], in_=b2[:, 0:n], func=AF.Ln, scale=1.0, bias=eps_ap),
                 reads=[b2k, "st"], writes=[rk_])
            P.op("scalar", lambda e: e.activation(out=r[:, 0:n], in_=r[:, 0:n], func=AF.Exp, scale=-0.5), reads=[rk_], writes=[rk_])
            dst_fn(r, rk_)

        def phaseA_gen(bb, bufs):
            pend_a = None
            for t in range(16):
                xa, xak = bufs[t % len(bufs)]
                P.op("sync", lambda e: e.dma_start(out=xa, in_=x_d[bb, t * 128:(t + 1) * 128, :]), writes=[xak], dma=xak)
                xb_, xkeys = norm_a(xa, xak)
                if pend_a is not None:
                    norm_b(*pend_a)
                pend_a = (xb_, xkeys, hT, "hT", t * 128, P_N1G)
                yield
            norm_b(*pend_a)

        xa_big = [(avf(i_ * 2048, 1024), "xa%d" % i_) for i_ in range(8)]
        xa_small = [(xt[0][:], "xt0"), (xt[1][:], "xt1"), (avf(57344, 1024), "xa8"), (avf(59392, 1024), "xa9")]

        for b in range(nseq):
            if b == 0 or not TUNE["aov"]:
                for _ in phaseA_gen(b, xa_big):
                    pass
            dump("hT", hT[:], "hT", lambda d: d.rearrange("(k p) t -> p k t", p=128))
            if upto == "A":
                break
            P.barrier()

            w0, w0k = wload(rows(w_in[:, C_RET:C_RET + 512]), 512)
            w1, w1k = wload(rows(w_in[:, C_RET + 512:C_RET + 1024]), 512)
            w2, w2k = wload(rows(w_in[:, C_RET + 1024:C_RET + 1536]), 512)
            for ch in range(2):
                for tt in range(4):
                    tsl = slice(tt * 512, (tt + 1) * 512)
                    bk, bkey = proj_fm(w0, w0k, ch * 128, tt)
                    P.op("scalar", lambda e, bk=bk, ch=ch, tsl=tsl: e.copy(out=rq[:, ch, tsl], in_=bk[:, 0:512]),
                         reads=[bkey], writes=["E"])
                    P.op("vector", lambda e, bk=bk, ch=ch, tsl=tsl: e.tensor_tensor(
                        out=qxf[:, ch, tsl].rearrange("p (n i) -> p n i", n=4), in0=bk[:, 0:512].rearrange("p (n i) -> p n i", n=4),
                        in1=xf[:, ch, :].unsqueeze(1).to_broadcast([128, 4, 128]), op=ALU.mult),
                        reads=[bkey, "rc"], writes=["S"])
                    P.op("vector", lambda e, bk=bk, ch=ch, tsl=tsl: e.tensor_tensor(
                        out=qxb[:, ch, tsl].rearrange("p (n i) -> p n i", n=4), in0=bk[:, 0:512].rearrange("p (n i) -> p n i", n=4),
                        in1=xb[:, ch, :].unsqueeze(1).to_broadcast([128, 4, 128]), op=ALU.mult),
                        reads=[bkey, "rc"], writes=["S"])
                    bk, bkey = proj_fm(w0, w0k, 256 + ch * 128, tt)
                    P.op("scalar", lambda e, bk=bk, ch=ch, tsl=tsl: e.copy(out=rk[:, ch, tsl], in_=bk[:, 0:512]),
                         reads=[bkey], writes=["E"])
            for c4 in range(4):
                for tt in range(4):
                    tsl = slice(tt * 512, (tt + 1) * 512)
                    bk, bkey = proj_fm(w2, w2k, c4 * 128, tt)
                    P.op("scalar", lambda e, bk=bk, c4=c4, tsl=tsl: e.activation(out=rgate[:, c4, tsl], in_=bk[:, 0:512], func=AF.Silu),
                         reads=[bkey], writes=["E"])
            for n in range(16):
                bk, bkey = bank()
                for kc in range(8):
                    P.op("tensor", lambda e, kc=kc, n=n, bk=bk: e.matmul(bk[:, 0:256], lhsT=hT[:, kc, n * 128:(n + 1) * 128],
                                                                          rhs=w0[:, kc, 256:512], start=(kc == 0), stop=(kc == 7)),
                         reads=[w0k, "hT"], writes=[bkey])
                P.op("vector", lambda e, n=n, bk=bk: e.tensor_tensor(out=kzf[:, n, :], in0=bk[:, 0:256], in1=zf, op=ALU.mult),
                     reads=[bkey, "rc"], writes=["Y"])
                P.op("vector", lambda e, n=n, bk=bk: e.tensor_tensor(out=kzb[:, n, :], in0=bk[:, 0:256], in1=zb, op=ALU.mult),
                     reads=[bkey, "rc"], writes=["Y"])
                bk, bkey = bank()
                for kc in range(8):
                    P.op("tensor", lambda e, kc=kc, n=n, bk=bk: e.matmul(bk[:, 0:512], lhsT=hT[:, kc, n * 128:(n + 1) * 128],
                                                                          rhs=w1[:, kc, 0:512], start=(kc == 0), stop=(kc == 7)),
                         reads=[w1k, "hT"], writes=[bkey])
                P.op("scalar", lambda e, n=n, bk=bk: e.copy(out=rv[:, n, :], in_=bk[:, 0:512]), reads=[bkey], writes=["Y"])
            pp = [xs[:].bitcast(F32), stt[:]]

            def scan_gen(ch, dr):
                cb_ = ch * 2 + dr
                kz, SA = (kzf, SfA) if dr == 0 else (kzb, SbA)
                order = list(range(16)) if dr == 0 else list(range(15, -1, -1))
                csl = slice(cb_ * 128, (cb_ + 1) * 128)
                P.op("vector", lambda e: e.memset(SA[:, ch, order[0], :], 0.0), writes=[("SA", cb_)])
                bk = bkey = None
                for pos in range(16):
                    n = order[pos]
                    li = pos % 4
                    if li == 0:
                        bk, bkey = bank()
                        for lj, n2 in enumerate(order[pos:pos + 4]):
                            for hh in range(2):
                                h = ch * 2 + hh
                                P.op("tensor", lambda e, lj=lj, n2=n2, hh=hh, h=h: e.matmul(
                                    bk[hh * 64:(hh + 1) * 64, lj * 128:(lj + 1) * 128], lhsT=kz[:, n2, h * 64:(h + 1) * 64],
                                    rhs=rv[:, n2, h * 128:(h + 1) * 128], start=True, stop=True),
                                    reads=["Y"], writes=[bkey])
                    if pos == 15:
                        break
                    cur, nxt_ = pp[pos % 2], pp[(pos + 1) % 2]
                    kcur, knxt = ("pp", pos % 2, cb_), ("pp", (pos + 1) % 2, cb_)
                    if pos == 0:
                        P.op("vector", lambda e: e.tensor_copy(out=nxt_[:, csl], in_=bk[:, li * 128:(li + 1) * 128]),
                             reads=[bkey], writes=[knxt])
                    else:
                        P.op("vector", lambda e: e.scalar_tensor_tensor(
                            out=nxt_[:, csl], in0=cur[:, csl], scalar=gC[:, cb_:cb_ + 1], in1=bk[:, li * 128:(li + 1) * 128],
                            op0=ALU.mult, op1=ALU.add), reads=[bkey, kcur, "rc"], writes=[knxt])
                    P.op("scalar", lambda e: e.copy(out=SA[:, ch, order[pos + 1], :], in_=nxt_[:, csl]),
                         reads=[knxt], writes=[("SA", cb_)])
                    yield

            def interleave(gens):
                gens = list(gens)
                while gens:
                    for gg in list(gens):
                        try:
                            next(gg)
                        except StopIteration:
                            gens.remove(gg)

            interleave([scan_gen(ch, dr) for ch in range(2) for dr in range(2)])
            P.op("vector", lambda e: e.memset(st[:, 6:7], 0.0),
                 reads=[("SA", 0), ("SA", 1), ("SA", 2), ("SA", 3)] + [("pp", a_, c_) for a_ in range(2) for c_ in range(4)],
                 writes=["S", "t32_3"])
            def y_gen(lane, heads):
                lb = [(ps[lane * 4 + i], "ps%d" % (lane * 4 + i)) for i in range(4)]
                ltb = [(tb[lane * 2 + i], "tb_%d" % (lane * 2 + i)) for i in range(2)]
                lt = [(t32[0], "t32_0"), (t32[1], "t32_1")] if lane == 0 else [(t32[2], "t32_2"), (stt, "t32_3")]
                for h in heads:
                    ch, hh = h // 2, h % 2
                    hp = slice(hh * 64, (hh + 1) * 64)
                    for tt in range(4):
                        (bs, bsk), (by, byk), (bm, bmk), (bv, bvk) = lb
                        (ta, tak), (tc, tck) = ltb
                        (m_sb, mk1), (msq, mk2) = lt
                        for li in range(4):
                            n = tt * 4 + li
                            nsl = slice(n * 128, (n + 1) * 128)
                            P.op("tensor", lambda e, li=li, nsl=nsl: e.matmul(
                                bs[:, li * 128:(li + 1) * 128], lhsT=rk[hp, ch, nsl], rhs=rq[hp, ch, nsl], start=True, stop=True),
                                reads=["E"], writes=[bsk])
                        P.op("vector", lambda e: e.tensor_tensor(
                            out=ta[:].rearrange("p (n i) -> p n i", n=4), in0=bs[:].rearrange("p (n i) -> p n i", n=4),
                            in1=dcomb[:, h, :].unsqueeze(1).to_broadcast([128, 4, 128]), op=ALU.mult),
                            reads=[bsk, "rc"], writes=[tak])
                        yield
                        for li in range(4):
                            n = tt * 4 + li
                            nsl = slice(n * 128, (n + 1) * 128)
                            csl = slice(li * 128, (li + 1) * 128)
                            P.op("tensor", lambda e, n=n, csl=csl: e.matmul(
                                by[:, csl], lhsT=rv[:, n, h * 128:(h + 1) * 128], rhs=ta[:, csl], start=True, stop=False),
                                reads=["Y", tak], writes=[byk])
                            P.op("tensor", lambda e, n=n, csl=csl, nsl=nsl: e.matmul(
                                by[:, csl], lhsT=SfA[hp, ch, n, :], rhs=qxf[hp, ch, nsl], start=False, stop=False),
                                reads=["S"], writes=[byk])
                            P.op("tensor", lambda e, n=n, csl=csl, nsl=nsl: e.matmul(
                                by[:, csl], lhsT=SbA[hp, ch, n, :], rhs=qxb[hp, ch, nsl], start=False, stop=True),
                                reads=["S"], writes=[byk])
                        yield
                        P.op("scalar", lambda e: e.copy(out=ta[:], in_=by[:]), reads=[byk], writes=[tak])
                        P.op("scalar", lambda e: e.activation(out=tc[:], in_=by[:], func=AF.Square), reads=[byk], writes=[tck])
                        P.op("tensor", lambda e: e.matmul(bm[:], lhsT=o128, rhs=ta[:], start=True, stop=True),
                             reads=["cb", tak], writes=[bmk])
                        P.op("tensor", lambda e: e.matmul(bv[:], lhsT=o128, rhs=tc[:], start=True, stop=True),
                             reads=["cb", tck], writes=[bvk])
                        yield
                        P.op("scalar", lambda e: e.copy(out=m_sb[:], in_=bm[:]), reads=[bmk], writes=[mk1])
                        P.op("scalar", lambda e: e.activation(out=msq[:], in_=bm[:], func=AF.Square), reads=[bmk], writes=[mk2])
                        P.op("vector", lambda e: e.tensor_tensor(out=msq[:], in0=bv[:], in1=msq[:], op=ALU.subtract),
                             reads=[bvk, mk2], writes=[mk2])
                        P.op("scalar", lambda e: e.activation(out=msq[:], in_=msq[:], func=AF.Ln, scale=1.0, bias=eps_ap),
                             reads=[mk2, "st"], writes=[mk2])
                        P.op("scalar", lambda e: e.activation(out=msq[:], in_=msq[:], func=AF.Exp, scale=-0.5), reads=[mk2], writes=[mk2])
                        yield
                        P.op("vector", lambda e: e.tensor_tensor(out=m_sb[:], in0=by[:], in1=m_sb[:], op=ALU.subtract),
                             reads=[byk, mk1], writes=[mk1])
                        P.op("vector", lambda e: e.scalar_tensor_tensor(
                            out=m_sb[:], in0=m_sb[:], scalar=par[:, P_GNG + h:P_GNG + h + 1], in1=msq[:], op0=ALU.mult, op1=ALU.mult),
                            reads=[mk1, mk2, "par"], writes=[mk1])
                        P.op("vector", lambda e: e.tensor_tensor(
                            out=yretT[:, h, tt * 512:(tt + 1) * 512], in0=m_sb[:], in1=rgate[:, h, tt * 512:(tt + 1) * 512], op=ALU.mult),
                            reads=[mk1, "E"], writes=["Yret"])
                        yield
            interleave([y_gen(0, (0, 1)), y_gen(1, (2, 3))])
            dump("yretT", yretT, "Yret", lambda d: d.rearrange("(c p) t -> p c t", p=128))
            if upto == "R":
                break
            P.barrier()

            for mt in range(2):
                i = nxt("xt", 2)
                P.op("sync", lambda e, i=i, mt=mt: e.dma_start(out=xt[i][:], in_=mem_d[b, mt * 128:(mt + 1) * 128, :]),
                     writes=["xt%d" % i], dma="xt%d" % i)
                norm_transpose(xt[i][:], "xt%d" % i, memnT, "memnT", mt * 128, P_MNG)
            wk_, wkk = wload(rows(w_kv[:, 0:512]), 512)
            wv_, wvk = wload(rows(w_kv[:, 512:1024]), 512)
            wq_, wqk = wload(rows(w_in[:, C_MEMQ:C_MEMQ + 512]), 512)
            for h in range(4):
                bk, bkey = proj_fm(wk_, wkk, h * 128, 0, rhsT=memnT, rkey="memnT", n=NMEM, toff=0)

                def dstk(r, rk_, bk=bk, bkey=bkey, h=h):
                    P.op("vector", lambda e: e.scalar_tensor_tensor(out=mk[:, h, :], in0=bk[:, 0:NMEM], scalar=par[:, P_MKG:P_MKG + 1],
                                                                    in1=r[:, 0:NMEM], op0=ALU.mult, op1=ALU.mult),
                         reads=[bkey, rk_, "par"], writes=["mk"])
                headnorm(bk, bkey, NMEM, o128, P_MKG, dstk, "mk")
            for mt in range(2):
                bk, bkey = bank()
                for kc in range(8):
                    P.op("tensor", lambda e, kc=kc, mt=mt, bk=bk: e.matmul(bk[:, 0:512], lhsT=memnT[:, kc, mt * 128:(mt + 1) * 128],
                                                                            rhs=wv_[:, kc, 0:512], start=(kc == 0), stop=(kc == 7)),
                         reads=[wvk, "memnT"], writes=[bkey])
                P.op("scalar", lambda e, mt=mt, bk=bk: e.copy(out=mv[:, mt, :], in_=bk[:, 0:512]), reads=[bkey], writes=["mv"])
            P.op("vector", lambda e: e.memset(st[:, 6:7], 0.0), reads=["xs", "tb_0", "tb_1", "tb_2", "tb_3"], writes=["xsA", "xsB"])

            def m_gen(lane, heads):
                (X0, X0k), (X1, X1k), (X2, X2k), (X3, X3k) = [(ps[lane * 4 + i_], "ps%d" % (lane * 4 + i_)) for i_ in range(4)]
                (T0, T0k), (T1, T1k) = [(tb[lane * 2 + i_], "tb_%d" % (lane * 2 + i_)) for i_ in range(2)]
                T2, T2k = (xs[:, 0:512], "xsA") if lane == 0 else (xs[:, 512:1024], "xsB")
                R0, R0k = t32[lane], "t32_%d" % lane
                for h in heads:
                    for tt in range(4):
                        for kc in range(8):
                            P.op("tensor", lambda e, kc=kc: e.matmul(X0[:, 0:512], lhsT=wq_[:, kc, h * 128:(h + 1) * 128],
                                                                      rhs=hT[:, kc, tt * 512:(tt + 1) * 512], start=(kc == 0), stop=(kc == 7)),
                                 reads=[wqk, "hT"], writes=[X0k])
                        P.op("scalar", lambda e: e.activation(out=T0[:], in_=X0[:, 0:512], func=AF.Square), reads=[X0k], writes=[T0k])
                        yield
                        P.op("tensor", lambda e: e.matmul(X1[:, 0:512], lhsT=o128, rhs=T0[:], start=True, stop=True),
                             reads=[T0k, "cb"], writes=[X1k])
                        P.op("scalar", lambda e: e.activation(out=R0[:], in_=X1[:, 0:512], func=AF.Ln, scale=1.0, bias=eps_ap),
                             reads=[X1k, "st"], writes=[R0k])
                        P.op("scalar", lambda e: e.activation(out=R0[:], in_=R0[:], func=AF.Exp, scale=-0.5), reads=[R0k], writes=[R0k])
                        P.op("vector", lambda e: e.scalar_tensor_tensor(out=T1[:], in0=X0[:, 0:512], scalar=par[:, P_MQG:P_MQG + 1],
                                                                        in1=R0[:], op0=ALU.mult, op1=ALU.mult),
                             reads=[X0k, R0k, "par"], writes=[T1k])
                        yield
                        for mt, (bs, bsk, pt, ptk) in enumerate(((X2, X2k, T0, T0k), (X3, X3k, T2, T2k))):
                            P.op("tensor", lambda e, mt=mt, bs=bs: e.matmul(bs[:, 0:512], lhsT=mk[:, h, mt * 128:(mt + 1) * 128], rhs=T1[:],
                                                                            start=True, stop=True), reads=["mk", T1k], writes=[bsk])
                            P.op("scalar", lambda e, bs=bs, pt=pt: e.activation(out=pt[:], in_=bs[:, 0:512], func=AF.Exp, scale=128.0 ** -0.5),
                                 reads=[bsk], writes=[ptk])
                        yield
                        for mt, (pt, ptk) in enumerate(((T0, T0k), (T2, T2k))):
                            P.op("tensor", lambda e, mt=mt, pt=pt: e.matmul(X0[:, 0:512], lhsT=mv[:, mt, h * 128:(h + 1) * 128], rhs=pt[:],
                                                                            start=(mt == 0), stop=(mt == 1)), reads=["mv", ptk], writes=[X0k])
                            P.op("tensor", lambda e, mt=mt, pt=pt: e.matmul(X1[:, 0:512], lhsT=ones, rhs=pt[:],
                                                                            start=(mt == 0), stop=(mt == 1)), reads=["cb", ptk], writes=[X1k])
                        P.op("scalar", lambda e: e.activation(out=R0[:], in_=X1[:, 0:512], func=AF.Ln), reads=[X1k], writes=[R0k])
                        P.op("scalar", lambda e: e.activation(out=R0[:], in_=R0[:], func=AF.Exp, scale=-1.0), reads=[R0k], writes=[R0k])
                        P.op("vector", lambda e: e.tensor_tensor(out=ymemT[:, h, tt * 512:(tt + 1) * 512], in0=X0[:, 0:512], in1=R0[:], op=ALU.mult),
                             reads=[X0k, R0k], writes=["Ymem"])
                        yield
            interleave([m_gen(0, (0, 1)), m_gen(1, (2, 3))])
            P.op("vector", lambda e: e.memset(st[:, 6:7], 0.0), reads=["xsA", "xsB"], writes=["xs"])
            dump("ymemT", ymemT, "Ymem", lambda d: d.rearrange("(c p) t -> p c t", p=128))
            if upto == "M":
                break
            P.barrier()

            pc = {"pb": 0, "sb": 0, "sq": 0, "r": 0, "tx": 0, "px": 0}

            def prep_begin(c, g, st_):
                r = DIL[g][1]
                n_sub = S // r
                cw = n_sub + 128
                dq = dqs[st_]
                src = [(rows(w_in[:, g * 1536 + j * 512 + c * 128:g * 1536 + j * 512 + (c + 1) * 128]),
                        (lambda d, j=j: d[:, :, j * 128:(j + 1) * 128])) for j in range(3)]
                w, wk = wload(src, 384)
                dqv = dq[:, 0:r * cw].rearrange("p (c i) -> p c i", c=r)
                P.op("vector", lambda e: e.memset(dqv[:, :, 0:64], 0.0), writes=["dq%d" % st_])
                P.op("vector", lambda e: e.memset(dqv[:, :, cw - 64:cw], 0.0), writes=["dq%d" % st_])
                for hh in range(2):
                    slope = 2.0 ** (-(2 * c + hh + 1))
                    P.op("scalar", lambda e: e.activation(out=Ets[st_][hh], in_=dist4, func=AF.Exp, scale=-slope * r),
                         reads=["cf"], writes=["Et%d_%d" % (st_, hh)])
                return w, wk

            def prep_gen(c, g, st_, w, wk):
                r = DIL[g][1]
                n_sub = S // r
                ntile = n_sub // 128
                cw = n_sub + 128
                dq, dk, dv = dqs[st_], dks[st_], dvs[st_]
                kq, kk, kv = "dq%d" % st_, "dk%d" % st_, "dv%d" % st_
                b2, b2k = ps[7], "ps7"
                dqv = dq[:, 0:r * cw].rearrange("p (c i) -> p c i", c=r)
                dkv = dk[:, 0:r * n_sub].rearrange("p (c i) -> p c i", c=r)
                tails = []
                for which in range(2):
                    coff = 0 if which == 0 else 128
                    gcol = (P_DQG if which == 0 else P_DKG) + g
                    for tt in range(4):
                        ib = pc["pb"] % 3
                        pc["pb"] += 1
                        bk, bkey = ps[ib], "ps%d" % ib
                        isq = pc["sq"] % 2
                        pc["sq"] += 1
                        sq, sqk = tb[isq], "tb_%d" % isq
                        rr, rrk = t32[isq], "t32_%d" % isq
                        for kc in range(8):
                            P.op("tensor", lambda e, kc=kc: e.matmul(bk[:, 0:512], lhsT=w[:, kc, coff:coff + 128],
                                                                      rhs=hT[:, kc, tt * 512:(tt + 1) * 512], start=(kc == 0), stop=(kc == 7)),
                                 reads=[wk, "hT"], writes=[bkey])
                        P.op("scalar", lambda e: e.activation(out=sq[:], in_=bk[:, 0:512], func=AF.Square), reads=[bkey], writes=[sqk])
                        i0 = (512 * tt) // r
                        ni = 512 // r
                        outv = dqv[:, :, 64 + i0:64 + i0 + ni] if which == 0 else dkv[:, :, i0:i0 + ni]

                        def tail(bk=bk, bkey=bkey, sq=sq, sqk=sqk, rr=rr, rrk=rrk, outv=outv, gcol=gcol, which=which):
                            P.op("tensor", lambda e: e.matmul(b2[:, 0:512], lhsT=bd64, rhs=sq[:], start=True, stop=True),
                                 reads=[sqk, "cb"], writes=[b2k])
                            P.op("scalar", lambda e: e.activation(out=rr[:], in_=b2[:, 0:512], func=AF.Ln, scale=1.0, bias=eps_ap),
                                 reads=[b2k, "st"], writes=[rrk])
                            P.op("scalar", lambda e: e.activation(out=rr[:], in_=rr[:], func=AF.Exp, scale=-0.5), reads=[rrk], writes=[rrk])
                            P.op("vector", lambda e: e.scalar_tensor_tensor(
                                out=outv, in0=bk[:, 0:512].rearrange("p (i c) -> p c i", c=r), scalar=par[:, gcol:gcol + 1],
                                in1=rr[:].rearrange("p (i c) -> p c i", c=r), op0=ALU.mult, op1=ALU.mult),
                                reads=[bkey, rrk, "par"], writes=[kq if which == 0 else kk])
                        tails.append(tail)
                        if len(tails) > 1:
                            tails.pop(0)()
                        yield
                while tails:
                    tails.pop(0)()
                for q4 in range(4):
                    ib = pc["pb"] % 3
                    pc["pb"] += 1
                    vb, vbk = ps[ib], "ps%d" % ib
                    for li in range(4):
                        ti = q4 * 4 + li
                        cl, m = divmod(ti, ntile)
                        t0 = cl + r * 128 * m
                        for kc in range(8):
                            P.op("tensor", lambda e, kc=kc, li=li, t0=t0: e.matmul(
                                vb[:, li * 128:(li + 1) * 128], lhsT=hT[:, kc, t0:t0 + r * 127 + 1:r], rhs=w[:, kc, 256:384],
                                start=(kc == 0), stop=(kc == 7)), reads=[wk, "hT"], writes=[vbk])
                    P.op("scalar", lambda e: e.copy(
                        out=dv[:, q4 * 4:(q4 + 1) * 4, :].rearrange("p n (b c) -> p n b c", b=3)[:, :, 0:3:2, :],
                        in_=vb[:].rearrange("p (n b c) -> p n b c", n=4, b=2)), reads=[vbk], writes=[kv])
                    yield

            def attn_gen(c, g, st_):
                r = DIL[g][1]
                n_sub = S // r
                ntile = n_sub // 128
                nq = ntile + 1
                cw = n_sub + 128
                dq, dk, dv = dqs[st_], dks[st_], dvs[st_]
                kq, kk, kv = "dq%d" % st_, "dk%d" % st_, "dv%d" % st_
                hb = [(ps[5], "ps5", accA, "accA"), (ps[6], "ps6", accB, "accB")]
                sbanks = (3, 4)
                pend = []

                def flush(keep):
                    while len(pend) > keep:
                        pend.pop(0)()

                if ntile == 1:
                    for cl0 in range(0, r, 4):
                        for hh in range(2):
                            hp = slice(hh * 64, (hh + 1) * 64)
                            isb = sbanks[pc["sb"] % 2]
                            pc["sb"] += 1
                            bs, bsk = ps[isb], "ps%d" % isb
                            for k4 in range(4):
                                cl = cl0 + k4
                                P.op("tensor", lambda e, k4=k4, cl=cl: e.matmul(
                                    bs[:, k4 * 128:(k4 + 1) * 128], lhsT=dk[hp, cl * 128:(cl + 1) * 128],
                                    rhs=dq[hp, cl * cw + 64:cl * cw + 192], start=True, stop=True), reads=[kq, kk], writes=[bsk])
                            ip = pc["px"] % 4
                            pc["px"] += 1
                            praw, prk = praws[ip], "praw_%d" % ip
                            pT, pTk = pTs[ip], "pT_%d" % ip
                            P.op("scalar", lambda e: e.activation(out=praw[:, :], in_=bs[:, :], func=AF.Exp, scale=0.125), reads=[bsk], writes=[prk])
                            P.op("vector", lambda e: e.tensor_tensor(
                                out=pT[:, :].rearrange("p (k i) -> p k i", k=4), in0=praw[:, :].rearrange("p (k i) -> p k i", k=4),
                                in1=Ets[st_][hh][:, 192:320].unsqueeze(1).to_broadcast([128, 4, 128]), op=ALU.mult),
                                reads=[prk, "Et%d_%d" % (st_, hh)], writes=[pTk])

                            def pv2(hh=hh, pT=pT, pTk=pTk, cl0=cl0):
                                bn, bnk, acc, akey = hb[hh]
                                vsl = slice(0, 128) if hh == 0 else slice(64, 192)
                                for k4 in range(4):
                                    P.op("tensor", lambda e, k4=k4: e.matmul(
                                        bn[:, k4 * 128:(k4 + 1) * 128], lhsT=dv[:, cl0 + k4, vsl], rhs=pT[:, k4 * 128:(k4 + 1) * 128],
                                        start=True, stop=True), reads=[kv, pTk], writes=[bnk])
                                accv = acc[:, :].rearrange("p (i c) -> p c i", c=r)[:, cl0:cl0 + 4, :]
                                P.op("vector", lambda e: e.tensor_tensor(out=accv, in0=accv, in1=bn[:, :].rearrange("p (k i) -> p k i", k=4),
                                                                         op=ALU.add), reads=[bnk, akey], writes=[akey])
                            pend.append(pv2)
                            flush(TUNE["lag"])
                            yield
                for cl in range(r if ntile > 1 else 0):
                    qbase = cl * cw
                    kbase = cl * n_sub
                    for j0 in range(0, nq, 4):
                        js = list(range(j0, min(j0 + 4, nq)))
                        npj = (len(js) + 1) // 2
                        for pi_, pj in enumerate(range(0, len(js), 2)):
                            pjs = js[pj:pj + 2]
                            for hh in range(2):
                                hp = slice(hh * 64, (hh + 1) * 64)
                                isb = sbanks[pc["sb"] % 2]
                                pc["sb"] += 1
                                bs, bsk = ps[isb], "ps%d" % isb
                                for qi, j in enumerate(pjs):
                                    qc = slice(qbase + 128 * j, qbase + 128 * j + 128)
                                    if j >= 1:
                                        kcs = slice(kbase + 128 * (j - 1), kbase + 128 * j)
                                        P.op("tensor", lambda e, qi=qi, qc=qc, kcs=kcs: e.matmul(
                                            bs[:, qi * 256:qi * 256 + 128], lhsT=dk[hp, kcs], rhs=dq[hp, qc], start=True, stop=True),
                                            reads=[kq, kk], writes=[bsk])
                                    if j < ntile:
                                        kcs = slice(kbase + 128 * j, kbase + 128 * (j + 1))
                                        P.op("tensor", lambda e, qi=qi, qc=qc, kcs=kcs: e.matmul(
                                            bs[:, qi * 256 + 128:qi * 256 + 256], lhsT=dk[hp, kcs], rhs=dq[hp, qc], start=True, stop=True),
                                            reads=[kq, kk], writes=[bsk])
                                lo = 128 if pjs[0] == 0 else 0
                                hi = len(pjs) * 256 - (128 if pjs[-1] == ntile else 0)
                                ip = pc["px"] % 4
                                pc["px"] += 1
                                praw, prk = praws[ip], "praw_%d" % ip
                                pT, pTk = pTs[ip], "pT_%d" % ip
                                P.op("scalar", lambda e, lo=lo, hi=hi: e.activation(out=praw[:, lo:hi], in_=bs[:, lo:hi], func=AF.Exp, scale=0.125),
                                     reads=[bsk], writes=[prk])
                                meng = "vector"
                                P.op(meng, lambda e, lo=lo, hi=hi: e.tensor_tensor(out=pT[:, lo:hi], in0=praw[:, lo:hi], in1=Ets[st_][hh][:, lo:hi],
                                                                                   op=ALU.mult),
                                     reads=[prk, "Et%d_%d" % (st_, hh)], writes=[pTk])
                                last_unit = (pi_ == npj - 1)

                                def pv(pjs=pjs, hh=hh, pT=pT, pTk=pTk, j0=j0, cl=cl, js=js, last_unit=last_unit):
                                    bn, bnk, acc, akey = hb[hh]
                                    vsl = slice(0, 128) if hh == 0 else slice(64, 192)
                                    for qi, j in enumerate(pjs):
                                        oc = slice((j - j0) * 128, (j - j0 + 1) * 128)
                                        tl = []
                                        if j >= 1:
                                            tl.append((j - 1, qi * 256))
                                        if j < ntile:
                                            tl.append((j, qi * 256 + 128))
                                        for ix, (m, pcx) in enumerate(tl):
                                            P.op("tensor", lambda e, oc=oc, m=m, pcx=pcx, ix=ix, n=len(tl): e.matmul(
                                                bn[:, oc], lhsT=dv[:, cl * ntile + m, vsl], rhs=pT[:, pcx:pcx + 128],
                                                start=(ix == 0), stop=(ix == n - 1)), reads=[kv, pTk], writes=[bnk])
                                    if last_unit:
                                        i_lo = max(128 * j0 - 64, 0)
                                        i_hi = min(128 * js[-1] + 64, n_sub)
                                        c_lo = i_lo - (128 * j0 - 64)
                                        ncol = i_hi - i_lo
                                        asl = slice(cl + r * i_lo, cl + r * (i_hi - 1) + 1, r)
                                        if g == 0:
                                            P.op("vector", lambda e: e.tensor_copy(out=acc[:, asl], in_=bn[:, c_lo:c_lo + ncol]),
                                                 reads=[bnk], writes=[akey])
                                        else:
                                            P.op("vector", lambda e: e.tensor_tensor(out=acc[:, asl], in0=acc[:, asl], in1=bn[:, c_lo:c_lo + ncol],
                                                                                     op=ALU.add), reads=[bnk, akey], writes=[akey])
                                pend.append(pv)
                                flush(TUNE["lag"])
                                yield
                flush(0)
                if g == 2:
                    for hf in range(2):
                        hsl = slice(hf * 1024, (hf + 1) * 1024)
                        P.op("scalar", lambda e: e.activation(out=tmpR[0:64, :], in_=accA[64:128, hsl], func=AF.Ln), reads=["accA"], writes=["tmpR"])
                        P.op("scalar", lambda e: e.activation(out=tmpR[64:128, :], in_=accB[0:64, hsl], func=AF.Ln), reads=["accB"], writes=["tmpR"])
                        P.op("scalar", lambda e: e.activation(out=tmpR[:, :], in_=tmpR[:, :], func=AF.Exp, scale=-1.0), reads=["tmpR"], writes=["tmpR"])
                        P.op("vector", lambda e: e.tensor_tensor(out=ydilT[0:64, c, hsl], in0=accA[0:64, hsl], in1=tmpR[0:64, :], op=ALU.mult),
                             reads=["accA", "tmpR"], writes=["Ydil"])
                        P.op("vector", lambda e: e.tensor_tensor(out=ydilT[64:128, c, hsl], in0=accB[64:128, hsl], in1=tmpR[64:128, :], op=ALU.mult),
                             reads=["accB", "tmpR"], writes=["Ydil"])

            def interleave(gens):
                gens = list(gens)
                while gens:
                    for gg in list(gens):
                        try:
                            next(gg)
                        except StopIteration:
                            gens.remove(gg)

            for st_ in range(2):
                P.op("vector", lambda e: e.memset(dvs[st_][:, :, 64:128], 1.0), writes=["dv%d" % st_])
            cgs = [(c, g) for c in range(4) for g in range(3)]

            def chain(*gs):
                for g_ in gs:
                    yield from g_

            def interleave_w(main, other, n_main, n_other):
                done_o = 0
                alive_o = other is not None
                i = 0
                while True:
                    try:
                        next(main)
                    except StopIteration:
                        break
                    i += 1
                    target = (i * n_other + max(n_main - 3, 1) - 1) // max(n_main - 3, 1)
                    while alive_o and done_o < target:
                        try:
                            next(other)
                            done_o += 1
                        except StopIteration:
                            alive_o = False
                while alive_o:
                    try:
                        next(other)
                    except StopIteration:
                        alive_o = False

            n_units = {0: 18, 1: 24, 2: 8}
            w_, wk_c = prep_begin(cgs[0][0], cgs[0][1], 0)
            interleave([prep_gen(cgs[0][0], cgs[0][1], 0, w_, wk_c)])
            for u, (c, g) in enumerate(cgs):
                other = None
                if u + 1 < len(cgs):
                    c2, g2 = cgs[u + 1]
                    w_, wk_c = prep_begin(c2, g2, (u + 1) % 2)
                    other = prep_gen(c2, g2, (u + 1) % 2, w_, wk_c)
                if TUNE["ilv"] == "even":
                    interleave_w(attn_gen(c, g, u % 2), other, n_units[g], 12)
                elif TUNE["ilv"] == "front":
                    interleave([attn_gen(c, g, u % 2)] + ([other] if other is not None else []))
                else:
                    if other is not None:
                        interleave([other])
                    interleave([attn_gen(c, g, u % 2)])
            dump("ydilT", ydilT, "Ydil", lambda d: d.rearrange("(c p) t -> p c t", p=128))
            if upto == "D":
                break
            P.barrier()

            wbr = []
            for bi, wsrc in enumerate((w_bd, w_br, w_bm)):
                sl_ = nxt("w", 3)
                wv_ = wbuf[sl_][:, 0:4096].rearrange("p (k n) -> p k n", k=4)
                P.op("gpsimd", lambda e: e.dma_start(out=wv_, in_=wsrc.rearrange("(k p) n -> p k n", p=128)),
                     writes=["wbuf%d" % sl_], dma="wbuf%d" % sl_)
                wbr.append((wv_, "wbuf%d" % sl_))
            ybs = ((ydilT, "Ydil"), (yretT, "Yret"), (ymemT, "Ymem"))
            s_keys = ["Et0_0", "Et0_1", "Et1_0", "Et1_1", "dq1", "dk1", "dv0", "dv1"] + ["praw_%d" % i_ for i_ in range(4)] + ["pT_%d" % i_ for i_ in range(4)]
            for oc in range(8):
                gsl = oc % 6
                wg = av(S0 + gsl * 3072, 3072).rearrange("p (k n) -> p k n", k=8)
                wgk = "gslot%d" % gsl
                for bi in range(3):
                    P.op("gpsimd", lambda e: e.dma_start(
                        out=wg[:, :, bi * 128:(bi + 1) * 128],
                        in_=rows(w_in[:, C_GATE + bi * 1024 + oc * 128:C_GATE + bi * 1024 + (oc + 1) * 128])),
                        writes=[wgk] + (s_keys if oc < 6 else []), dma=wgk)
                for tt in range(4):
                    tsl = slice(tt * 512, (tt + 1) * 512)
                    for bi in range(3):
                        bg, bgk = proj_fm(wg, wgk, bi * 128, tt)
                        sg = t32[1 + (bi % 2)]
                        sgk = "t32_%d" % (1 + (bi % 2))
                        P.op("scalar", lambda e, bg=bg, sg=sg: e.activation(out=sg[:], in_=bg[:, 0:512], func=AF.Sigmoid),
                             reads=[bgk], writes=[sgk])
                        bp, bpk = bank()
                        yb, ybk = ybs[bi]
                        for kc in range(4):
                            P.op("tensor", lambda e, kc=kc, bp=bp, bi=bi, oc=oc, yb=yb, tsl=tsl: e.matmul(
                                bp[:, 0:512], lhsT=wbr[bi][0][:, kc, oc * 128:(oc + 1) * 128], rhs=yb[:, kc, tsl],
                                start=(kc == 0), stop=(kc == 3)), reads=[wbr[bi][1], ybk], writes=[bpk])
                        if bi == 0:
                            P.op("vector", lambda e, bp=bp, sg=sg: e.tensor_tensor(out=t32[0][:], in0=sg[:], in1=bp[:, 0:512], op=ALU.mult),
                                 reads=[bpk, sgk], writes=["t32_0"])
                        else:
                            P.op("vector", lambda e, bp=bp, sg=sg: e.tensor_tensor(out=sg[:], in0=sg[:], in1=bp[:, 0:512], op=ALU.mult),
                                 reads=[bpk, sgk], writes=[sgk])
                            if bi == 1:
                                P.op("vector", lambda e, sg=sg: e.tensor_tensor(out=t32[0][:], in0=t32[0][:], in1=sg[:], op=ALU.add),
                                     reads=["t32_0", sgk], writes=["t32_0"])
                            else:
                                P.op("vector", lambda e, sg=sg, oc=oc, tsl=tsl: e.tensor_tensor(out=mergedT[:, oc, tsl], in0=t32[0][:], in1=sg[:],
                                                                                                op=ALU.add),
                                     reads=["t32_0", sgk], writes=["E"])
            dump("mergedT", mergedT, "E", lambda d: d.rearrange("(c p) t -> p c t", p=128))
            if upto == "G":
                break

            wo = [wload(rows(w_out[:, hf * 512:(hf + 1) * 512]), 512) for hf in range(2)]
            pend_o = None
            obufs = [(xt[0][:], "xt0", []), (xt[1][:], "xt1", []), (avf(0, 1024), "xa0", ["Yret"]), (avf(2048, 1024), "xa1", ["Yret"])]

            def oload(t):
                xo, xok, extra = obufs[t % 4]
                P.op("sync", lambda e: e.dma_start(out=xo, in_=x_d[b, t * 128:(t + 1) * 128, :]), writes=[xok] + extra, dma=xok)
            for t in range(3):
                oload(t)
            for t in range(16):
                if t + 3 < 16:
                    oload(t + 3)
                xo, xok, _ = obufs[t % 4]
                for hf in range(2):
                    bk, bkey = bank()
                    for kc in range(8):
                        P.op("tensor", lambda e, kc=kc: e.matmul(
                            bk[:, 0:512], lhsT=mergedT[:, kc, t * 128:(t + 1) * 128], rhs=wo[hf][0][:, kc, 0:512],
                            start=(kc == 0), stop=(kc == 7)), reads=["E", wo[hf][1]], writes=[bkey])
                    P.op("vector", lambda e: e.tensor_tensor(out=xo[:, hf * 512:(hf + 1) * 512], in0=xo[:, hf * 512:(hf + 1) * 512],
                                                             in1=bk[:, 0:512], op=ALU.add), reads=[bkey, xok], writes=[xok])
                P.op("sync", lambda e: e.dma_start(out=out_d[b, t * 128:(t + 1) * 128, :], in_=xo),
                     reads=[xok], writes=[("x2d", b, t, 0), ("x2d", b, t, 1)], dma=xok, is_out=True)
                xb_, xkeys = norm_a(xo, xok)
                if pend_o is not None:
                    norm_b(*pend_o)
                pend_o = (xb_, xkeys, hT, "hT", t * 128, P_N2G)
            norm_b(*pend_o)
            dump("h2T", hT[:], "hT", lambda d: d.rearrange("(k p) t -> p k t", p=128))
            if upto == "O":
                break
            P.barrier()

            P.op("vector", lambda e: e.memset(ubuf[:, 0:1], 0.0), writes=["ubuf"])
            P.op("vector", lambda e: e.memset(ubuf[:, 2049:2050], 0.0), writes=["ubuf"])
            for jp in range(11):
                src = [(rows(w_f1[:, sx * DFF + jp * 256:sx * DFF + (jp + 1) * 256]),
                        (lambda d, sx=sx: d[:, :, sx * 256:(sx + 1) * 256])) for sx in range(2)]
                w, wk = wload(src, 512)
                for jj in range(2):
                    j = 2 * jp + jj
                    for tt in range(4):
                        bu, buk = proj_fm(w, wk, jj * 128, tt)
                        P.op("scalar", lambda e, bu=bu, tt=tt: e.copy(out=ubuf[:, 1 + tt * 512:1 + (tt + 1) * 512], in_=bu[:, 0:512]),
                             reads=[buk], writes=["ubuf"])
                    cwc = lambda i, j=j: par[:, P_CW + i * NJ + j:P_CW + i * NJ + j + 1]
                    P.op("vector", lambda e, j=j, cwc=cwc: e.tensor_scalar(out=cbuf[:, :], in0=ubuf[:, 1:2049], scalar1=cwc(1),
                                                                           scalar2=par[:, P_CB + j:P_CB + j + 1], op0=ALU.mult, op1=ALU.add),
                         reads=["ubuf", "par"], writes=["cbuf"])
                    P.op("vector", lambda e, cwc=cwc: e.scalar_tensor_tensor(out=cbuf[:, :], in0=ubuf[:, 0:2048], scalar=cwc(0), in1=cbuf[:, :],
                                                                             op0=ALU.mult, op1=ALU.add), reads=["ubuf", "cbuf", "par"], writes=["cbuf"])
                    P.op("vector", lambda e, cwc=cwc: e.scalar_tensor_tensor(out=cbuf[:, :], in0=ubuf[:, 2:2050], scalar=cwc(2), in1=cbuf[:, :],
                                                                             op0=ALU.mult, op1=ALU.add), reads=["ubuf", "cbuf", "par"], writes=["cbuf"])
                    P.op("scalar", lambda e: e.activation(out=cbuf[:, :], in_=cbuf[:, :], func=AF.Gelu), reads=["cbuf"], writes=["cbuf"])
                    for tt in range(4):
                        bg, bgk = proj_fm(w, wk, 256 + jj * 128, tt)
                        P.op("vector", lambda e, bg=bg, tt=tt, j=j: e.tensor_tensor(out=yT[:, j, tt * 512:(tt + 1) * 512],
                                                                                    in0=cbuf[:, tt * 512:(tt + 1) * 512], in1=bg[:, 0:512], op=ALU.mult),
                             reads=[bgk, "cbuf"], writes=["yT"])
            dump("yT", yT, "yT", lambda d: d.rearrange("(j p) t -> p j t", p=128))
            if upto == "F":
                break
            def ffn_out_gen(b=b):
                tslots = [av(45056 + i_ * 4096, 4096).rearrange("p (k n) -> p k n", k=8) for i_ in range(3)]
                for hf in range(2):
                    chunks = []
                    for ci, (j0, j1) in enumerate(((0, 8), (8, 16), (16, 22))):
                        nk = j1 - j0
                        srcw = w_f2[j0 * 128:j1 * 128, hf * 512:(hf + 1) * 512].rearrange("(k p) n -> p k n", p=128)
                        if hf == 0:
                            w, wk = wload(srcw, 512, view=lambda d, nk=nk: d[:, 0:nk, :])
                        else:
                            w, wk = tslots[ci], "w2s%d" % ci
                            P.op("gpsimd", lambda e: e.dma_start(out=w[:, 0:nk, :], in_=srcw), writes=[wk, "ubuf", "cbuf"], dma=wk)
                        chunks.append((j0, j1, w, wk))
                    for tt in range(4):
                        bks = [bank() for _ in range(4)]
                        for (j0, j1, w, wk) in chunks:
                            for tl in range(4):
                                t = tt * 4 + tl
                                for j in range(j0, j1):
                                    P.op("tensor", lambda e, bk=bks[tl][0], j=j, t=t: e.matmul(
                                        bk[:, 0:512], lhsT=yT[:, j, t * 128:(t + 1) * 128], rhs=w[:, j - j0, 0:512],
                                        start=(j == 0), stop=(j == NJ - 1)), reads=["yT", wk], writes=[bks[tl][1]])
                        for tl in range(4):
                            t = tt * 4 + tl
                            it = nxt("t32", 3)
                            dsl = out_d[b, t * 128:(t + 1) * 128, hf * 512:(hf + 1) * 512]
                            P.op("sync", lambda e: e.dma_start(out=t32[it][:], in_=dsl),
                                 reads=[("x2d", b, t, hf)], writes=["t32_%d" % it], dma="t32_%d" % it)
                            P.op("vector", lambda e, bk=bks[tl][0]: e.tensor_tensor(out=t32[it][:], in0=t32[it][:], in1=bk[:, 0:512], op=ALU.add),
                                 reads=[bks[tl][1], "t32_%d" % it], writes=["t32_%d" % it])
                            P.op("sync", lambda e: e.dma_start(out=dsl, in_=t32[it][:]),
                                 reads=["t32_%d" % it], writes=[("x2d", b, t, hf)], dma="t32_%d" % it, is_out=True)
                        yield

            if b + 1 < nseq and TUNE["aov"]:
                interleave_w(phaseA_gen(b + 1, xa_small), ffn_out_gen(), 16, 8)
            else:
                for _ in ffn_out_gen():
                    pass
    P.emit()
    return nc, P


def _host_consts():
    c = np.zeros((128, NCF), np.float32)
    c[:, K_ID:K_ID + 128] = np.eye(128, dtype=np.float32)
    bd = np.zeros((128, 128), np.float32)
    bd[:64, :64] = 1.0 / 64
    bd[64:, 64:] = 1.0 / 64
    c[:, K_BD:K_BD + 128] = bd
    c[:, K_O128:K_O128 + 128] = 1.0 / 128
    c[:, K_ONE:K_ONE + 128] = 1.0
    kl = np.arange(128)[:, None].astype(np.float32)
    ql = np.arange(128)[None, :].astype(np.float32)
    BIG = 1.0e6
    A = np.where(kl >= ql, np.abs(kl - ql - 64), BIG)
    Bm = np.where(kl <= ql, np.abs(kl - ql + 64), BIG)
    c[:, K_DIST:K_DIST + 512] = np.concatenate([A, Bm, A, Bm], axis=1)
    j = kl
    i = ql
    c[:, K_DPOS:K_DPOS + 128] = np.maximum(i - j, 0)
    c[:, K_DNEG:K_DNEG + 128] = np.maximum(j - i, 0)
    c[:, K_MF:K_MF + 128] = (i >= j)
    c[:, K_MB:K_MB + 128] = (j > i)
    c[:, K_ZF] = 127 - np.arange(128)
    c[:, K_ZB] = np.arange(128)
    c[:, K_XF:K_XF + 128] = np.arange(128)[None, :] + 1
    c[:, K_XB:K_XB + 128] = 128 - np.arange(128)[None, :]
    return c


def _host_params(inp):
    p = np.zeros((128, NP), np.float32)
    p[:, P_N1G:P_N1G + 8] = inp["norm1_g"][0].reshape(8, 128).T
    p[:, P_MNG:P_MNG + 8] = inp["mem_norm_g"][0].reshape(8, 128).T
    p[:, P_N2G:P_N2G + 8] = inp["norm2_g"][0].reshape(8, 128).T
    p[:, P_DQG:P_DQG + 3] = np.tile(inp["dil_q_norm_g"][0].T, (2, 1))
    p[:, P_DKG:P_DKG + 3] = np.tile(inp["dil_k_norm_g"][0].T, (2, 1))
    p[:, P_MQG] = inp["mem_q_norm_g"][0]
    p[:, P_MKG] = inp["mem_k_norm_g"][0]
    p[:, P_GNG:P_GNG + 4] = inp["ret_gn_g"][0].reshape(4, 128).T
    p[:, P_CW:P_CW + 66] = inp["ffn_conv_w"][0].reshape(3, NJ, 128).transpose(2, 0, 1).reshape(128, 66)
    p[:, P_CB:P_CB + NJ] = inp["ffn_conv_b"][0].reshape(NJ, 128).T
    lgt = inp["ret_decay_logit"][0]
    p[:, P_LG:P_LG + 8] = np.tile(lgt.reshape(1, 8), (128, 1))
    for ch in range(2):
        for dr in range(2):
            p[:64, P_LGP + ch * 2 + dr] = lgt[dr, 2 * ch]
            p[64:, P_LGP + ch * 2 + dr] = lgt[dr, 2 * ch + 1]
    return p


_CACHE = {}


def _prep_maps(inputs, nseq, ncores):
    f = lambda a: np.ascontiguousarray(np.asarray(a, dtype=np.float32))
    shared = {
        "w_in": f(inputs["w_in"][0]), "w_mem_kv": f(inputs["w_mem_kv"][0]),
        "w_branch_dil": f(inputs["w_branch_dil"][0]), "w_branch_ret": f(inputs["w_branch_ret"][0]),
        "w_branch_mem": f(inputs["w_branch_mem"][0]), "w_out": f(inputs["w_out"][0]),
        "w_ffn_in": f(inputs["w_ffn_in"][0]), "w_ffn_out": f(inputs["w_ffn_out"][0]),
        "params": _host_params({k: np.asarray(v) for k, v in inputs.items()}), "consts": _host_consts(),
    }
    maps = []
    for c in range(ncores):
        m = dict(shared)
        m["x"] = f(inputs["x"][c * nseq:(c + 1) * nseq])
        m["mem"] = f(inputs["mem"][c * nseq:(c + 1) * nseq])
        maps.append(m)
    return maps


def kernel(**inputs):
    if "nc" not in _CACHE:
        _CACHE["nc"] = build(NSEQ)[0]
    nc = _CACHE["nc"]
    maps = _prep_maps(inputs, NSEQ, NCORES)
    res = run_bass_kernel_spmd(nc, maps, core_ids=list(range(NCORES)))
    return np.concatenate([r["out"] for r in res.results], axis=0).astype(np.float32)
```

```python
import types
import numpy as np
from contextlib import ExitStack
import concourse.bass as bass
import concourse.mybir as mybir
from concourse.bass_utils import run_bass_kernel_spmd

F32 = mybir.dt.float32
BF16 = mybir.dt.bfloat16
AF = mybir.ActivationFunctionType
ALU = mybir.AluOpType

S = 2048
D = 1024
NCORES = 8
NSEQ = 4
NMEM = 256
DFF = 2816
NJ = 22
EPS = 1e-6
DIL = ((128, 1), (512, 4), (2048, 16))
C_DIL = 4608
C_RET = 4608
C_MEMQ = 6144
C_GATE = 6656

P_N1G, P_MNG, P_N2G = 0, 8, 16
P_DQG, P_DKG = 24, 27
P_MQG, P_MKG = 30, 31
P_GNG = 32
P_CW = 36
P_CB = 102
P_LG = 124
P_LGP = 132
NP = 136

K_ID, K_BD, K_O128, K_ONE = 0, 128, 256, 384
K_DIST = 512
K_DPOS, K_DNEG, K_MF, K_MB = 1024, 1152, 1280, 1408
K_ZF, K_ZB = 1536, 1537
K_XF, K_XB = 1538, 1666
NCF = 1794


def _freeze(fn):
    if fn is None or fn.__closure__ is None:
        return fn
    cells = []
    for c in fn.__closure__:
        try:
            cells.append(types.CellType(c.cell_contents))
        except ValueError:
            cells.append(c)
    return types.FunctionType(fn.__code__, fn.__globals__, fn.__name__, fn.__defaults__, tuple(cells))


class Prog:
    ENG = ("tensor", "vector", "scalar", "gpsimd", "sync")

    def __init__(self, nc, ctx):
        self.nc = nc
        self.ctx = ctx
        self.ops = []
        self.res = {}
        self.last = {e: None for e in self.ENG}
        self.bar = {e: set() for e in self.ENG}
        self.out_dmas = []
        self.last_dma = {}

    def _r(self, key):
        if key not in self.res:
            self.res[key] = {"w": None, "r": []}
        return self.res[key]

    def op(self, eng, fn, reads=(), writes=(), dma=None, is_out=False):
        idx = len(self.ops)
        deps = {}

        def add(d, kind):
            if d not in deps or kind < deps[d]:
                deps[d] = kind
        psum_reads = [k for k in reads if isinstance(k, str) and k.startswith("ps")]
        for k in reads:
            r = self._r(k)
            if r["w"] is not None:
                add(r["w"], 0)
        for k in writes:
            r = self._r(k)
            if r["w"] is not None:
                add(r["w"], 1)
            for t in r["r"]:
                add(t, 1)
        for k in psum_reads:
            r = self._r(k)
            for t in r["r"]:
                add(t, 2)
        for d in self.bar[eng]:
            add(d, 0)
        self.bar[eng] = set()
        deps.pop(idx, None)
        self.ops.append({"eng": eng, "fn": _freeze(fn), "deps": deps, "dma": dma})
        for k in reads:
            self._r(k)["r"].append(idx)
        for k in writes:
            r = self._r(k)
            r["w"] = idx
            r["r"] = []
        self.last[eng] = idx
        if dma is not None:
            self.last_dma[dma] = idx
        if is_out:
            self.out_dmas.append(idx)
        return idx

    def barrier(self, engines=("tensor", "vector", "scalar", "sync")):
        lasts = set(self.last[e] for e in engines if self.last[e] is not None)
        lasts |= set(v for k, v in self.last_dma.items() if isinstance(k, str) and k.startswith("xa"))
        for e in engines:
            self.bar[e] |= lasts

    def emit(self):
        nc = self.nc
        ops = self.ops
        idx = len(ops)
        ops.append({"eng": "sync", "fn": None, "deps": {d: 0 for d in self.out_dmas}, "dma": None})
        needed = [False] * len(ops)
        for i, o in enumerate(ops):
            keep = set()
            for d, kind in o["deps"].items():
                po = ops[d]
                if po["dma"] is None and o["dma"] is None and po["eng"] == o["eng"]:
                    if o["eng"] == "tensor" or kind == 2:
                        continue
                keep.add(d)
            o["deps"] = keep
            for d in keep:
                needed[d] = True
        eng_sem = {}
        eng_cnt = {e: 0 for e in self.ENG}
        for e in self.ENG:
            eng_sem[e] = self.ctx.enter_context(nc.semaphore("sem_" + e))
        dma_sem = {}
        dma_cnt = {}
        for i, o in enumerate(ops):
            if o["dma"] is not None:
                k = o["dma"]
                if k not in dma_sem:
                    dma_sem[k] = self.ctx.enter_context(nc.semaphore("dsem_%d" % len(dma_sem)))
                    dma_cnt[k] = 0
                dma_cnt[k] += 16
                o["tok"] = (dma_sem[k], dma_cnt[k])
                o["inc"] = (dma_sem[k], 16)
            elif needed[i]:
                eng_cnt[o["eng"]] += 1
                o["tok"] = (eng_sem[o["eng"]], eng_cnt[o["eng"]])
                o["inc"] = (eng_sem[o["eng"]], 1)
            else:
                o["tok"] = None
                o["inc"] = None
        self.stats = {"ops": len(ops), "eng_cnt": dict(eng_cnt), "dma_sems": len(dma_sem),
                      "per_eng": {e: sum(1 for o in ops if o["eng"] == e) for e in self.ENG}}
        by_eng = {e: [] for e in self.ENG}
        for i, o in enumerate(ops):
            by_eng[o["eng"]].append(i)

        def run(e, name):
            waited = {}
            for i in by_eng[name]:
                o = ops[i]
                w = {}
                for d in o["deps"]:
                    sem, val = ops[d]["tok"]
                    key = id(sem)
                    if key not in w or w[key][1] < val:
                        w[key] = (sem, val)
                for key, (sem, val) in w.items():
                    if waited.get(key, 0) >= val:
                        continue
                    waited[key] = val
                    e.wait_ge(sem, val)
                if o["fn"] is None:
                    continue
                ins = o["fn"](e)
                if o["inc"] is not None:
                    ins.then_inc(o["inc"][0], o["inc"][1])

        with nc.Block() as block:
            @block.tensor
            def _(e):
                run(e, "tensor")

            @block.vector
            def _(e):
                run(e, "vector")

            @block.scalar
            def _(e):
                run(e, "scalar")

            @block.gpsimd
            def _(e):
                run(e, "gpsimd")

            @block.sync
            def _(e):
                run(e, "sync")


TUNE = {"ilv": "front", "lag": 2, "aov": False}


def build(nseq=NSEQ, upto="all", dbg=()):
    nc = bass.Bass("TRN2", target_bir_lowering=False)

    def din(name, shape):
        return nc.dram_tensor(name, list(shape), F32, kind="ExternalInput").ap()

    x_d = din("x", [nseq, S, D])
    mem_d = din("mem", [nseq, NMEM, D])
    w_in = din("w_in", [D, 9728])
    w_kv = din("w_mem_kv", [D, 1024])
    w_bd = din("w_branch_dil", [512, D])
    w_br = din("w_branch_ret", [512, D])
    w_bm = din("w_branch_mem", [512, D])
    w_out = din("w_out", [D, D])
    w_f1 = din("w_ffn_in", [D, 2 * DFF])
    w_f2 = din("w_ffn_out", [DFF, D])
    par_d = din("params", [128, NP])
    cst_d = din("consts", [128, NCF])
    out_d = nc.dram_tensor("out", [nseq, S, D], F32, kind="ExternalOutput").ap()
    dbg_d = {}
    for name, shape in dbg:
        dbg_d[name] = nc.dram_tensor("dbg_" + name, list(shape), F32, kind="ExternalOutput").ap()

    with ExitStack() as ctx:
        P = Prog(nc, ctx)

        def sb(name, shape, dt):
            return ctx.enter_context(nc.sbuf_tensor(name, list(shape), dt))

        par = sb("par", [128, NP], F32)
        cf = sb("cf", [128, 1024 - 512 + 0], F32)
        cb = sb("cb", [128, 512], BF16)
        rc = sb("rc", [128, 4 * 128 + 256 + 256 + 2 * 256 + 8], F32)
        lg = sb("lg", [128, 16], F32)
        hT = sb("hT", [128, 8, S], BF16)
        wbuf = [sb("wbuf%d" % i, [128, 8 * 512], BF16) for i in range(3)]
        arena = sb("arena", [128, 61440], BF16)
        xt = [sb("xt%d" % i, [128, D], F32) for i in range(2)]
        xs = sb("xs", [128, D], BF16)
        st = sb("st", [128, 8], F32)
        t32 = [sb("t32_%d" % i, [128, 512], F32) for i in range(3)]
        tball = sb("tball", [128, 2048], BF16)
        tb = [tball[:, i * 512:(i + 1) * 512] for i in range(4)]
        stt = sb("stt", [128, 512], F32)
        ps = [ctx.enter_context(nc.psum_tensor("ps%d" % i, [128, 512], F32)) for i in range(8)]

        def av(off, n):
            return arena[:, off:off + n]

        def avf(off, n):
            return arena[:, off:off + 2 * n].bitcast(F32)

        Y0, E0, S0 = 0, 24576, 40960
        yretT = av(Y0, 8192).rearrange("p (c t) -> p c t", c=4)
        ymemT = av(Y0 + 8192, 8192).rearrange("p (c t) -> p c t", c=4)
        ydilT = av(Y0 + 16384, 8192).rearrange("p (c t) -> p c t", c=4)
        rv = av(Y0 + 8192, 8192).rearrange("p (n c) -> p n c", n=16)
        kzf = av(Y0 + 16384, 4096).rearrange("p (n c) -> p n c", n=16)
        kzb = av(Y0 + 20480, 4096).rearrange("p (n c) -> p n c", n=16)
        rq = av(E0, 4096).rearrange("p (c t) -> p c t", c=2)
        rk = av(E0 + 4096, 4096).rearrange("p (c t) -> p c t", c=2)
        rgate = av(E0 + 8192, 8192).rearrange("p (c t) -> p c t", c=4)
        SfA = av(S0, 4096).rearrange("p (c n v) -> p c n v", c=2, n=16)
        SbA = av(S0 + 4096, 4096).rearrange("p (c n v) -> p c n v", c=2, n=16)
        S32 = avf(S0 + 8192, 2048).rearrange("p (n v) -> p n v", n=16)
        qxf = av(S0 + 12288, 4096).rearrange("p (c t) -> p c t", c=2)
        qxb = av(S0 + 16384, 4096).rearrange("p (c t) -> p c t", c=2)
        memnT = av(S0, 2048).rearrange("p (k t) -> p k t", k=8)
        mk = av(S0 + 2048, 1024).rearrange("p (h t) -> p h t", h=4)
        mv = av(S0 + 3072, 1024).rearrange("p (m c) -> p m c", m=2)
        dqs = [av(E0, 4096), av(S0 + 2048, 4096)]
        dks = [av(E0 + 4096, 2048), av(S0 + 6144, 2048)]
        dvs = [av(S0 + 8192, 3072).rearrange("p (n c) -> p n c", n=16), av(S0 + 11264, 3072).rearrange("p (n c) -> p n c", n=16)]
        accA = avf(E0 + 6144, 2048)
        accB = avf(E0 + 10240, 2048)
        tmpR = avf(E0 + 14336, 1024)
        Ets = [[av(S0 + (st_ * 2 + hh) * 512, 512) for hh in range(2)] for st_ in range(2)]
        praws = [av(S0 + 14336 + i * 512, 512) for i in range(4)]
        pTs = [av(S0 + 16384 + i * 512, 512) for i in range(4)]
        mergedT = av(E0, 16384).rearrange("p (c t) -> p c t", c=8)
        wbr = [av(S0 + i * 4096, 4096).rearrange("p (k n) -> p k n", k=4) for i in range(3)]
        yT = av(0, 45056).rearrange("p (j t) -> p j t", j=NJ)
        ubuf = avf(45056, 2050)
        cbuf = avf(45056 + 4100, 2048)

        ident = cb[:, 0:128]
        bd64 = cb[:, 128:256]
        o128 = cb[:, 256:384]
        ones = cb[:, 384:512]
        dist4 = cf[:, 0:512]
        dcomb = rc[:, 0:512].rearrange("p (h i) -> p h i", h=4)
        zf = rc[:, 512:768]
        zb = rc[:, 768:1024]
        xf = rc[:, 1024:1280].rearrange("p (c i) -> p c i", c=2)
        xb = rc[:, 1280:1536].rearrange("p (c i) -> p c i", c=2)
        gC = rc[:, 1536:1540]

        cnt = {"bank": 0, "w": 0, "t32": 0, "tb": 0, "xt": 0, "nt": 0}

        def bank():
            i = cnt["bank"] % 8
            cnt["bank"] += 1
            return ps[i], "ps%d" % i

        def nxt(kind, n):
            i = cnt[kind] % n
            cnt[kind] += 1
            return i

        def wload(src_ap, ncols, view=None):
            s = nxt("w", 3)
            dst = wbuf[s][:, 0:8 * ncols].rearrange("p (k n) -> p k n", k=8)
            pairs = src_ap if isinstance(src_ap, list) else [(src_ap, view)]
            for sa, vw in pairs:
                P.op("gpsimd", lambda e, sa=sa, vw=vw: e.dma_start(out=dst if vw is None else vw(dst), in_=sa),
                     writes=["wbuf%d" % s], dma="wbuf%d" % s)
            return dst, "wbuf%d" % s

        def rows(ap):
            return ap.rearrange("(k p) n -> p k n", p=128)

        def dump(name, src_ap, key, dram_view=None):
            if name in dbg_d:
                dst = dbg_d[name] if dram_view is None else dram_view(dbg_d[name])
                P.op("gpsimd", lambda e: e.dma_start(out=dst, in_=src_ap), reads=[key], dma="dbg_" + name, is_out=True)

        P.op("sync", lambda e: e.dma_start(out=par[:], in_=par_d), writes=["par"], dma="par")
        P.op("sync", lambda e: e.dma_start(out=cf[:], in_=cst_d[:, K_DIST:K_DIST + 512]), writes=["cf"], dma="cf")
        P.op("gpsimd", lambda e: e.dma_start(out=cb[:], in_=cst_d[:, 0:512]), writes=["cb"], dma="cb")
        P.op("sync", lambda e: e.dma_start(out=rc[:, 0:512], in_=cst_d[:, K_DPOS:K_DPOS + 512]), writes=["rc"], dma="rc")
        P.op("scalar", lambda e: e.activation(out=lg[:, 0:12], in_=par[:, P_LG:P_LG + 12], func=AF.Exp, scale=-1.0),
             reads=["par"], writes=["lg"])
        P.op("scalar", lambda e: e.activation(out=lg[:, 0:12], in_=lg[:, 0:12], func=AF.Ln, bias=1.0, scale=1.0),
             reads=["lg"], writes=["lg"])
        P.op("scalar", lambda e: e.mul(out=lg[:, 0:12], in_=lg[:, 0:12], mul=-1.0), reads=["lg"], writes=["lg"])
        P.op("vector", lambda e: e.tensor_copy(out=t32[0][:], in_=rc[:, 0:512]), reads=["rc"], writes=["t32_0"])
        for h in range(4):
            P.op("scalar", lambda e, h=h: e.activation(out=t32[1][:, 0:128], in_=t32[0][:, 0:128], func=AF.Exp,
                                                        scale=lg[:, h:h + 1]), reads=["t32_0", "lg"], writes=["t32_1"])
            P.op("scalar", lambda e, h=h: e.activation(out=t32[1][:, 128:256], in_=t32[0][:, 128:256], func=AF.Exp,
                                                        scale=lg[:, 4 + h:5 + h]), reads=["t32_0", "lg"], writes=["t32_1"])
            P.op("vector", lambda e: e.tensor_tensor(out=t32[1][:, 0:256], in0=t32[1][:, 0:256], in1=t32[0][:, 256:512],
                                                     op=ALU.mult), reads=["t32_1", "t32_0"], writes=["t32_1"])
            P.op("vector", lambda e, h=h: e.tensor_tensor(out=dcomb[:, h, :], in0=t32[1][:, 0:128], in1=t32[1][:, 128:256],
                                                           op=ALU.add), reads=["t32_1"], writes=["rc"])
            P.op("vector", lambda e, h=h: e.tensor_scalar(out=dcomb[:, h, :], in0=dcomb[:, h, :], scalar1=0.125, scalar2=None,
                                                           op0=ALU.mult), reads=["rc"], writes=["rc"])
        P.op("sync", lambda e: e.dma_start(out=t32[2][:, 0:2], in_=cst_d[:, K_ZF:K_ZF + 2]), writes=["t32_2"], dma="t32_2")
        for h in range(4):
            for dr, dst_t in ((0, zf), (1, zb)):
                P.op("scalar", lambda e, h=h, dr=dr, dst_t=dst_t: e.activation(
                    out=dst_t[:, h * 64:(h + 1) * 64], in_=t32[2][:, dr:dr + 1].to_broadcast([128, 64]), func=AF.Exp,
                    scale=lg[:, dr * 4 + h:dr * 4 + h + 1]), reads=["t32_2", "lg"], writes=["rc"])
        P.op("vector", lambda e: e.tensor_scalar(out=rc[:, 512:1024], in0=rc[:, 512:1024], scalar1=0.125, scalar2=None,
                                                 op0=ALU.mult), reads=["rc"], writes=["rc"])
        P.op("sync", lambda e: e.dma_start(out=t32[2][:, 128:384], in_=cst_d[:, K_XF:K_XF + 256]), writes=["t32_2"], dma="t32_2")
        for ch in range(2):
            P.op("scalar", lambda e, ch=ch: e.activation(out=xf[:, ch, :], in_=t32[2][:, 128:256], func=AF.Exp,
                                                          scale=lg[:, 8 + ch * 2:9 + ch * 2]), reads=["t32_2", "lg"], writes=["rc"])
            P.op("scalar", lambda e, ch=ch: e.activation(out=xb[:, ch, :], in_=t32[2][:, 256:384], func=AF.Exp,
                                                          scale=lg[:, 9 + ch * 2:10 + ch * 2]), reads=["t32_2", "lg"], writes=["rc"])
        P.op("scalar", lambda e: e.activation(out=gC, in_=lg[:, 8:12], func=AF.Exp, scale=128.0), reads=["lg"], writes=["rc"])

        def rstd_from_ss(ss_ap, n, scale, key):
            P.op("scalar", lambda e: e.activation(out=ss_ap, in_=ss_ap, func=AF.Ln, scale=scale, bias=eps_ap),
                 reads=[key, "st"], writes=[key])
            P.op("scalar", lambda e: e.activation(out=ss_ap, in_=ss_ap, func=AF.Exp, scale=-0.5), reads=[key], writes=[key])

        eps_ap = st[:, 7:8]
        P.op("vector", lambda e: e.memset(st[:, 7:8], EPS), writes=["st"])

        def norm_a(src, src_key):
            si = nxt("nt", 2)
            if si == 0:
                xb_, xkeys = xs[:], ["xs"]
            else:
                xb_, xkeys = tball[:, 0:1024], ["tb_0", "tb_1"]
            sc, sk = st[:, si:si + 1], "st%d" % si
            P.op("scalar", lambda e: e.activation(out=xb_, in_=src, func=AF.Square, accum_out=sc),
                 reads=[src_key], writes=xkeys + [sk])
            P.op("scalar", lambda e: e.activation(out=sc, in_=sc, func=AF.Ln, scale=1.0 / D, bias=eps_ap),
                 reads=[sk, "st"], writes=[sk])
            P.op("scalar", lambda e: e.activation(out=sc, in_=sc, func=AF.Exp, scale=-0.5), reads=[sk], writes=[sk])
            P.op("vector", lambda e: e.tensor_scalar(out=xb_, in0=src, scalar1=sc, scalar2=None, op0=ALU.mult),
                 reads=[src_key, sk], writes=xkeys)
            return xb_, xkeys

        def norm_b(xb_, xkeys, dstT, dst_key, tcol, gcol):
            bk, bkey = bank()
            pstv = bk[:].bitcast(BF16)
            for kc in range(8):
                P.op("tensor", lambda e, kc=kc: e.transpose(out=pstv[:, kc * 128:(kc + 1) * 128],
                                                             in_=xb_[:, kc * 128:(kc + 1) * 128], identity=ident),
                     reads=xkeys + ["cb"], writes=[bkey])
            P.op("vector", lambda e: e.tensor_tensor(
                out=dstT[:, :, tcol:tcol + 128], in0=pstv.rearrange("p (k i) -> p k i", k=8),
                in1=par[:, gcol:gcol + 8].unsqueeze(2).to_broadcast([128, 8, 128]), op=ALU.mult),
                reads=[bkey, "par"], writes=[dst_key])

        def norm_transpose(src, src_key, dstT, dst_key, tcol, gcol):
            xb_, xkeys = norm_a(src, src_key)
            norm_b(xb_, xkeys, dstT, dst_key, tcol, gcol)

        def proj_fm(w, wkey, coff, tt, rhsT=hT, rkey="hT", n=512, toff=None):
            bk, bkey = bank()
            t0 = tt * 512 if toff is None else toff
            for kc in range(8):
                P.op("tensor", lambda e, kc=kc: e.matmul(bk[:, 0:n], lhsT=w[:, kc, coff:coff + 128],
                                                          rhs=rhsT[:, kc, t0:t0 + n], start=(kc == 0), stop=(kc == 7)),
                     reads=[wkey, rkey], writes=[bkey])
            return bk, bkey

        def headnorm(bk, bkey, n, onesm, gcol, dst_fn, dkey):
            i1 = nxt("tb", 4)
            sq, sqk = tb[i1], "tb_%d" % i1
            P.op("scalar", lambda e: e.activation(out=sq[:, 0:n], in_=bk[:, 0:n], func=AF.Square), reads=[bkey], writes=[sqk])
            b2, b2k = bank()
            P.op("tensor", lambda e: e.matmul(b2[:, 0:n], lhsT=onesm, rhs=sq[:, 0:n], start=True, stop=True),
                 reads=[sqk, "cb"], writes=[b2k])
            i2 = nxt("t32", 3)
            r, rk_ = t32[i2], "t32_%d" % i2
            P.op("scalar", lambda e: e.activation(out=r[:, 0:n], in_=b2[:, 0:n], func=AF.Ln, scale=1.0, bias=eps_ap),
                 reads=[b2k, "st"], writes=[rk_])
            P.op("scalar", lambda e: e.activation(out=r[:, 0:n], in_=r[:, 0:n], func=AF.Exp, scale=-0.5), reads=[rk_], writes=[rk_])
            dst_fn(r, rk_)

        def phaseA_gen(bb, bufs):
            pend_a = None
            for t in range(16):
                xa, xak = bufs[t % len(bufs)]
                P.op("sync", lambda e: e.dma_start(out=xa, in_=x_d[bb, t * 128:(t + 1) * 128, :]), writes=[xak], dma=xak)
                xb_, xkeys = norm_a(xa, xak)
                if pend_a is not None:
                    norm_b(*pend_a)
                pend_a = (xb_, xkeys, hT, "hT", t * 128, P_N1G)
                yield
            norm_b(*pend_a)

        xa_big = [(avf(i_ * 2048, 1024), "xa%d" % i_) for i_ in range(8)]
        xa_small = [(xt[0][:], "xt0"), (xt[1][:], "xt1"), (avf(57344, 1024), "xa8"), (avf(59392, 1024), "xa9")]

        for b in range(nseq):
            if b == 0 or not TUNE["aov"]:
                for _ in phaseA_gen(b, xa_big):
                    pass
            dump("hT", hT[:], "hT", lambda d: d.rearrange("(k p) t -> p k t", p=128))
            if upto == "A":
                break
            P.barrier()

            w0, w0k = wload(rows(w_in[:, C_RET:C_RET + 512]), 512)
            w1, w1k = wload(rows(w_in[:, C_RET + 512:C_RET + 1024]), 512)
            w2, w2k = wload(rows(w_in[:, C_RET + 1024:C_RET + 1536]), 512)
            for ch in range(2):
                for tt in range(4):
                    tsl = slice(tt * 512, (tt + 1) * 512)
                    bk, bkey = proj_fm(w0, w0k, ch * 128, tt)
                    P.op("scalar", lambda e, bk=bk, ch=ch, tsl=tsl: e.copy(out=rq[:, ch, tsl], in_=bk[:, 0:512]),
                         reads=[bkey], writes=["E"])
                    P.op("vector", lambda e, bk=bk, ch=ch, tsl=tsl: e.tensor_tensor(
                        out=qxf[:, ch, tsl].rearrange("p (n i) -> p n i", n=4), in0=bk[:, 0:512].rearrange("p (n i) -> p n i", n=4),
                        in1=xf[:, ch, :].unsqueeze(1).to_broadcast([128, 4, 128]), op=ALU.mult),
                        reads=[bkey, "rc"], writes=["S"])
                    P.op("vector", lambda e, bk=bk, ch=ch, tsl=tsl: e.tensor_tensor(
                        out=qxb[:, ch, tsl].rearrange("p (n i) -> p n i", n=4), in0=bk[:, 0:512].rearrange("p (n i) -> p n i", n=4),
                        in1=xb[:, ch, :].unsqueeze(1).to_broadcast([128, 4, 128]), op=ALU.mult),
                        reads=[bkey, "rc"], writes=["S"])
                    bk, bkey = proj_fm(w0, w0k, 256 + ch * 128, tt)
                    P.op("scalar", lambda e, bk=bk, ch=ch, tsl=tsl: e.copy(out=rk[:, ch, tsl], in_=bk[:, 0:512]),
                         reads=[bkey], writes=["E"])
            for c4 in range(4):
                for tt in range(4):
                    tsl = slice(tt * 512, (tt + 1) * 512)
                    bk, bkey = proj_fm(w2, w2k, c4 * 128, tt)
                    P.op("scalar", lambda e, bk=bk, c4=c4, tsl=tsl: e.activation(out=rgate[:, c4, tsl], in_=bk[:, 0:512], func=AF.Silu),
                         reads=[bkey], writes=["E"])
            for n in range(16):
                bk, bkey = bank()
                for kc in range(8):
                    P.op("tensor", lambda e, kc=kc, n=n, bk=bk: e.matmul(bk[:, 0:256], lhsT=hT[:, kc, n * 128:(n + 1) * 128],
                                                                          rhs=w0[:, kc, 256:512], start=(kc == 0), stop=(kc == 7)),
                         reads=[w0k, "hT"], writes=[bkey])
                P.op("vector", lambda e, n=n, bk=bk: e.tensor_tensor(out=kzf[:, n, :], in0=bk[:, 0:256], in1=zf, op=ALU.mult),
                     reads=[bkey, "rc"], writes=["Y"])
                P.op("vector", lambda e, n=n, bk=bk: e.tensor_tensor(out=kzb[:, n, :], in0=bk[:, 0:256], in1=zb, op=ALU.mult),
                     reads=[bkey, "rc"], writes=["Y"])
                bk, bkey = bank()
                for kc in range(8):
                    P.op("tensor", lambda e, kc=kc, n=n, bk=bk: e.matmul(bk[:, 0:512], lhsT=hT[:, kc, n * 128:(n + 1) * 128],
                                                                          rhs=w1[:, kc, 0:512], start=(kc == 0), stop=(kc == 7)),
                         reads=[w1k, "hT"], writes=[bkey])
                P.op("scalar", lambda e, n=n, bk=bk: e.copy(out=rv[:, n, :], in_=bk[:, 0:512]), reads=[bkey], writes=["Y"])
            pp = [xs[:].bitcast(F32), stt[:]]

            def scan_gen(ch, dr):
                cb_ = ch * 2 + dr
                kz, SA = (kzf, SfA) if dr == 0 else (kzb, SbA)
                order = list(range(16)) if dr == 0 else list(range(15, -1, -1))
                csl = slice(cb_ * 128, (cb_ + 1) * 128)
                P.op("vector", lambda e: e.memset(SA[:, ch, order[0], :], 0.0), writes=[("SA", cb_)])
                bk = bkey = None
                for pos in range(16):
                    n = order[pos]
                    li = pos % 4
                    if li == 0:
                        bk, bkey = bank()
                        for lj, n2 in enumerate(order[pos:pos + 4]):
                            for hh in range(2):
                                h = ch * 2 + hh
                                P.op("tensor", lambda e, lj=lj, n2=n2, hh=hh, h=h: e.matmul(
                                    bk[hh * 64:(hh + 1) * 64, lj * 128:(lj + 1) * 128], lhsT=kz[:, n2, h * 64:(h + 1) * 64],
                                    rhs=rv[:, n2, h * 128:(h + 1) * 128], start=True, stop=True),
                                    reads=["Y"], writes=[bkey])
                    if pos == 15:
                        break
                    cur, nxt_ = pp[pos % 2], pp[(pos + 1) % 2]
                    kcur, knxt = ("pp", pos % 2, cb_), ("pp", (pos + 1) % 2, cb_)
                    if pos == 0:
                        P.op("vector", lambda e: e.tensor_copy(out=nxt_[:, csl], in_=bk[:, li * 128:(li + 1) * 128]),
                             reads=[bkey], writes=[knxt])
                    else:
                        P.op("vector", lambda e: e.scalar_tensor_tensor(
                            out=nxt_[:, csl], in0=cur[:, csl], scalar=gC[:, cb_:cb_ + 1], in1=bk[:, li * 128:(li + 1) * 128],
                            op0=ALU.mult, op1=ALU.add), reads=[bkey, kcur, "rc"], writes=[knxt])
                    P.op("scalar", lambda e: e.copy(out=SA[:, ch, order[pos + 1], :], in_=nxt_[:, csl]),
                         reads=[knxt], writes=[("SA", cb_)])
                    yield

            def interleave(gens):
                gens = list(gens)
                while gens:
                    for gg in list(gens):
                        try:
                            next(gg)
                        except StopIteration:
                            gens.remove(gg)

            interleave([scan_gen(ch, dr) for ch in range(2) for dr in range(2)])
            P.op("vector", lambda e: e.memset(st[:, 6:7], 0.0),
                 reads=[("SA", 0), ("SA", 1), ("SA", 2), ("SA", 3)] + [("pp", a_, c_) for a_ in range(2) for c_ in range(4)],
                 writes=["S", "t32_3"])
            def y_gen(lane, heads):
                lb = [(ps[lane * 4 + i], "ps%d" % (lane * 4 + i)) for i in range(4)]
                ltb = [(tb[lane * 2 + i], "tb_%d" % (lane * 2 + i)) for i in range(2)]
                lt = [(t32[0], "t32_0"), (t32[1], "t32_1")] if lane == 0 else [(t32[2], "t32_2"), (stt, "t32_3")]
                for h in heads:
                    ch, hh = h // 2, h % 2
                    hp = slice(hh * 64, (hh + 1) * 64)
                    for tt in range(4):
                        (bs, bsk), (by, byk), (bm, bmk), (bv, bvk) = lb
                        (ta, tak), (tc, tck) = ltb
                        (m_sb, mk1), (msq, mk2) = lt
                        for li in range(4):
                            n = tt * 4 + li
                            nsl = slice(n * 128, (n + 1) * 128)
                            P.op("tensor", lambda e, li=li, nsl=nsl: e.matmul(
                                bs[:, li * 128:(li + 1) * 128], lhsT=rk[hp, ch, nsl], rhs=rq[hp, ch, nsl], start=True, stop=True),
                                reads=["E"], writes=[bsk])
                        P.op("vector", lambda e: e.tensor_tensor(
                            out=ta[:].rearrange("p (n i) -> p n i", n=4), in0=bs[:].rearrange("p (n i) -> p n i", n=4),
                            in1=dcomb[:, h, :].unsqueeze(1).to_broadcast([128, 4, 128]), op=ALU.mult),
                            reads=[bsk, "rc"], writes=[tak])
                        yield
                        for li in range(4):
                            n = tt * 4 + li
                            nsl = slice(n * 128, (n + 1) * 128)
                            csl = slice(li * 128, (li + 1) * 128)
                            P.op("tensor", lambda e, n=n, csl=csl: e.matmul(
                                by[:, csl], lhsT=rv[:, n, h * 128:(h + 1) * 128], rhs=ta[:, csl], start=True, stop=False),
                                reads=["Y", tak], writes=[byk])
                            P.op("tensor", lambda e, n=n, csl=csl, nsl=nsl: e.matmul(
                                by[:, csl], lhsT=SfA[hp, ch, n, :], rhs=qxf[hp, ch, nsl], start=False, stop=False),
                                reads=["S"], writes=[byk])
                            P.op("tensor", lambda e, n=n, csl=csl, nsl=nsl: e.matmul(
                                by[:, csl], lhsT=SbA[hp, ch, n, :], rhs=qxb[hp, ch, nsl], start=False, stop=True),
                                reads=["S"], writes=[byk])
                        yield
                        P.op("scalar", lambda e: e.copy(out=ta[:], in_=by[:]), reads=[byk], writes=[tak])
                        P.op("scalar", lambda e: e.activation(out=tc[:], in_=by[:], func=AF.Square), reads=[byk], writes=[tck])
                        P.op("tensor", lambda e: e.matmul(bm[:], lhsT=o128, rhs=ta[:], start=True, stop=True),
                             reads=["cb", tak], writes=[bmk])
                        P.op("tensor", lambda e: e.matmul(bv[:], lhsT=o128, rhs=tc[:], start=True, stop=True),
                             reads=["cb", tck], writes=[bvk])
                        yield
                        P.op("scalar", lambda e: e.copy(out=m_sb[:], in_=bm[:]), reads=[bmk], writes=[mk1])
                        P.op("scalar", lambda e: e.activation(out=msq[:], in_=bm[:], func=AF.Square), reads=[bmk], writes=[mk2])
                        P.op("vector", lambda e: e.tensor_tensor(out=msq[:], in0=bv[:], in1=msq[:], op=ALU.subtract),
                             reads=[bvk, mk2], writes=[mk2])
                        P.op("scalar", lambda e: e.activation(out=msq[:], in_=msq[:], func=AF.Ln, scale=1.0, bias=eps_ap),
                             reads=[mk2, "st"], writes=[mk2])
                        P.op("scalar", lambda e: e.activation(out=msq[:], in_=msq[:], func=AF.Exp, scale=-0.5), reads=[mk2], writes=[mk2])
                        yield
                        P.op("vector", lambda e: e.tensor_tensor(out=m_sb[:], in0=by[:], in1=m_sb[:], op=ALU.subtract),
                             reads=[byk, mk1], writes=[mk1])
                        P.op("vector", lambda e: e.scalar_tensor_tensor(
                            out=m_sb[:], in0=m_sb[:], scalar=par[:, P_GNG + h:P_GNG + h + 1], in1=msq[:], op0=ALU.mult, op1=ALU.mult),
                            reads=[mk1, mk2, "par"], writes=[mk1])
                        P.op("vector", lambda e: e.tensor_tensor(
                            out=yretT[:, h, tt * 512:(tt + 1) * 512], in0=m_sb[:], in1=rgate[:, h, tt * 512:(tt + 1) * 512], op=ALU.mult),
                            reads=[mk1, "E"], writes=["Yret"])
                        yield
            interleave([y_gen(0, (0, 1)), y_gen(1, (2, 3))])
            dump("yretT", yretT, "Yret", lambda d: d.rearrange("(c p) t -> p c t", p=128))
            if upto == "R":
                break
            P.barrier()

            for mt in range(2):
                i = nxt("xt", 2)
                P.op("sync", lambda e, i=i, mt=mt: e.dma_start(out=xt[i][:], in_=mem_d[b, mt * 128:(mt + 1) * 128, :]),
                     writes=["xt%d" % i], dma="xt%d" % i)
                norm_transpose(xt[i][:], "xt%d" % i, memnT, "memnT", mt * 128, P_MNG)
            wk_, wkk = wload(rows(w_kv[:, 0:512]), 512)
            wv_, wvk = wload(rows(w_kv[:, 512:1024]), 512)
            wq_, wqk = wload(rows(w_in[:, C_MEMQ:C_MEMQ + 512]), 512)
            for h in range(4):
                bk, bkey = proj_fm(wk_, wkk, h * 128, 0, rhsT=memnT, rkey="memnT", n=NMEM, toff=0)

                def dstk(r, rk_, bk=bk, bkey=bkey, h=h):
                    P.op("vector", lambda e: e.scalar_tensor_tensor(out=mk[:, h, :], in0=bk[:, 0:NMEM], scalar=par[:, P_MKG:P_MKG + 1],
                                                                    in1=r[:, 0:NMEM], op0=ALU.mult, op1=ALU.mult),
                         reads=[bkey, rk_, "par"], writes=["mk"])
                headnorm(bk, bkey, NMEM, o128, P_MKG, dstk, "mk")
            for mt in range(2):
                bk, bkey = bank()
                for kc in range(8):
                    P.op("tensor", lambda e, kc=kc, mt=mt, bk=bk: e.matmul(bk[:, 0:512], lhsT=memnT[:, kc, mt * 128:(mt + 1) * 128],
                                                                            rhs=wv_[:, kc, 0:512], start=(kc == 0), stop=(kc == 7)),
                         reads=[wvk, "memnT"], writes=[bkey])
                P.op("scalar", lambda e, mt=mt, bk=bk: e.copy(out=mv[:, mt, :], in_=bk[:, 0:512]), reads=[bkey], writes=["mv"])
            P.op("vector", lambda e: e.memset(st[:, 6:7], 0.0), reads=["xs", "tb_0", "tb_1", "tb_2", "tb_3"], writes=["xsA", "xsB"])

            def m_gen(lane, heads):
                (X0, X0k), (X1, X1k), (X2, X2k), (X3, X3k) = [(ps[lane * 4 + i_], "ps%d" % (lane * 4 + i_)) for i_ in range(4)]
                (T0, T0k), (T1, T1k) = [(tb[lane * 2 + i_], "tb_%d" % (lane * 2 + i_)) for i_ in range(2)]
                T2, T2k = (xs[:, 0:512], "xsA") if lane == 0 else (xs[:, 512:1024], "xsB")
                R0, R0k = t32[lane], "t32_%d" % lane
                for h in heads:
                    for tt in range(4):
                        for kc in range(8):
                            P.op("tensor", lambda e, kc=kc: e.matmul(X0[:, 0:512], lhsT=wq_[:, kc, h * 128:(h + 1) * 128],
                                                                      rhs=hT[:, kc, tt * 512:(tt + 1) * 512], start=(kc == 0), stop=(kc == 7)),
                                 reads=[wqk, "hT"], writes=[X0k])
                        P.op("scalar", lambda e: e.activation(out=T0[:], in_=X0[:, 0:512], func=AF.Square), reads=[X0k], writes=[T0k])
                        yield
                        P.op("tensor", lambda e: e.matmul(X1[:, 0:512], lhsT=o128, rhs=T0[:], start=True, stop=True),
                             reads=[T0k, "cb"], writes=[X1k])
                        P.op("scalar", lambda e: e.activation(out=R0[:], in_=X1[:, 0:512], func=AF.Ln, scale=1.0, bias=eps_ap),
                             reads=[X1k, "st"], writes=[R0k])
                        P.op("scalar", lambda e: e.activation(out=R0[:], in_=R0[:], func=AF.Exp, scale=-0.5), reads=[R0k], writes=[R0k])
                        P.op("vector", lambda e: e.scalar_tensor_tensor(out=T1[:], in0=X0[:, 0:512], scalar=par[:, P_MQG:P_MQG + 1],
                                                                        in1=R0[:], op0=ALU.mult, op1=ALU.mult),
                             reads=[X0k, R0k, "par"], writes=[T1k])
                        yield
                        for mt, (bs, bsk, pt, ptk) in enumerate(((X2, X2k, T0, T0k), (X3, X3k, T2, T2k))):
                            P.op("tensor", lambda e, mt=mt, bs=bs: e.matmul(bs[:, 0:512], lhsT=mk[:, h, mt * 128:(mt + 1) * 128], rhs=T1[:],
                                                                            start=True, stop=True), reads=["mk", T1k], writes=[bsk])
                            P.op("scalar", lambda e, bs=bs, pt=pt: e.activation(out=pt[:], in_=bs[:, 0:512], func=AF.Exp, scale=128.0 ** -0.5),
                                 reads=[bsk], writes=[ptk])
                        yield
                        for mt, (pt, ptk) in enumerate(((T0, T0k), (T2, T2k))):
                            P.op("tensor", lambda e, mt=mt, pt=pt: e.matmul(X0[:, 0:512], lhsT=mv[:, mt, h * 128:(h + 1) * 128], rhs=pt[:],
                                                                            start=(mt == 0), stop=(mt == 1)), reads=["mv", ptk], writes=[X0k])
                            P.op("tensor", lambda e, mt=mt, pt=pt: e.matmul(X1[:, 0:512], lhsT=ones, rhs=pt[:],
                                                                            start=(mt == 0), stop=(mt == 1)), reads=["cb", ptk], writes=[X1k])
                        P.op("scalar", lambda e: e.activation(out=R0[:], in_=X1[:, 0:512], func=AF.Ln), reads=[X1k], writes=[R0k])
                        P.op("scalar", lambda e: e.activation(out=R0[:], in_=R0[:], func=AF.Exp, scale=-1.0), reads=[R0k], writes=[R0k])
                        P.op("vector", lambda e: e.tensor_tensor(out=ymemT[:, h, tt * 512:(tt + 1) * 512], in0=X0[:, 0:512], in1=R0[:], op=ALU.mult),
                             reads=[X0k, R0k], writes=["Ymem"])
                        yield
            interleave([m_gen(0, (0, 1)), m_gen(1, (2, 3))])
            P.op("vector", lambda e: e.memset(st[:, 6:7], 0.0), reads=["xsA", "xsB"], writes=["xs"])
            dump("ymemT", ymemT, "Ymem", lambda d: d.rearrange("(c p) t -> p c t", p=128))
            if upto == "M":
                break
            P.barrier()

            pc = {"pb": 0, "sb": 0, "sq": 0, "r": 0, "tx": 0, "px": 0}

            def prep_begin(c, g, st_):
                r = DIL[g][1]
                n_sub = S // r
                cw = n_sub + 128
                dq = dqs[st_]
                src = [(rows(w_in[:, g * 1536 + j * 512 + c * 128:g * 1536 + j * 512 + (c + 1) * 128]),
                        (lambda d, j=j: d[:, :, j * 128:(j + 1) * 128])) for j in range(3)]
                w, wk = wload(src, 384)
                dqv = dq[:, 0:r * cw].rearrange("p (c i) -> p c i", c=r)
                P.op("vector", lambda e: e.memset(dqv[:, :, 0:64], 0.0), writes=["dq%d" % st_])
                P.op("vector", lambda e: e.memset(dqv[:, :, cw - 64:cw], 0.0), writes=["dq%d" % st_])
                for hh in range(2):
                    slope = 2.0 ** (-(2 * c + hh + 1))
                    P.op("scalar", lambda e: e.activation(out=Ets[st_][hh], in_=dist4, func=AF.Exp, scale=-slope * r),
                         reads=["cf"], writes=["Et%d_%d" % (st_, hh)])
                return w, wk

            def prep_gen(c, g, st_, w, wk):
                r = DIL[g][1]
                n_sub = S // r
                ntile = n_sub // 128
                cw = n_sub + 128
                dq, dk, dv = dqs[st_], dks[st_], dvs[st_]
                kq, kk, kv = "dq%d" % st_, "dk%d" % st_, "dv%d" % st_
                b2, b2k = ps[7], "ps7"
                dqv = dq[:, 0:r * cw].rearrange("p (c i) -> p c i", c=r)
                dkv = dk[:, 0:r * n_sub].rearrange("p (c i) -> p c i", c=r)
                tails = []
                for which in range(2):
                    coff = 0 if which == 0 else 128
                    gcol = (P_DQG if which == 0 else P_DKG) + g
                    for tt in range(4):
                        ib = pc["pb"] % 3
                        pc["pb"] += 1
                        bk, bkey = ps[ib], "ps%d" % ib
                        isq = pc["sq"] % 2
                        pc["sq"] += 1
                        sq, sqk = tb[isq], "tb_%d" % isq
                        rr, rrk = t32[isq], "t32_%d" % isq
                        for kc in range(8):
                            P.op("tensor", lambda e, kc=kc: e.matmul(bk[:, 0:512], lhsT=w[:, kc, coff:coff + 128],
                                                                      rhs=hT[:, kc, tt * 512:(tt + 1) * 512], start=(kc == 0), stop=(kc == 7)),
                                 reads=[wk, "hT"], writes=[bkey])
                        P.op("scalar", lambda e: e.activation(out=sq[:], in_=bk[:, 0:512], func=AF.Square), reads=[bkey], writes=[sqk])
                        i0 = (512 * tt) // r
                        ni = 512 // r
                        outv = dqv[:, :, 64 + i0:64 + i0 + ni] if which == 0 else dkv[:, :, i0:i0 + ni]

                        def tail(bk=bk, bkey=bkey, sq=sq, sqk=sqk, rr=rr, rrk=rrk, outv=outv, gcol=gcol, which=which):
                            P.op("tensor", lambda e: e.matmul(b2[:, 0:512], lhsT=bd64, rhs=sq[:], start=True, stop=True),
                                 reads=[sqk, "cb"], writes=[b2k])
                            P.op("scalar", lambda e: e.activation(out=rr[:], in_=b2[:, 0:512], func=AF.Ln, scale=1.0, bias=eps_ap),
                                 reads=[b2k, "st"], writes=[rrk])
                            P.op("scalar", lambda e: e.activation(out=rr[:], in_=rr[:], func=AF.Exp, scale=-0.5), reads=[rrk], writes=[rrk])
                            P.op("vector", lambda e: e.scalar_tensor_tensor(
                                out=outv, in0=bk[:, 0:512].rearrange("p (i c) -> p c i", c=r), scalar=par[:, gcol:gcol + 1],
                                in1=rr[:].rearrange("p (i c) -> p c i", c=r), op0=ALU.mult, op1=ALU.mult),
                                reads=[bkey, rrk, "par"], writes=[kq if which == 0 else kk])
                        tails.append(tail)
                        if len(tails) > 1:
                            tails.pop(0)()
                        yield
                while tails:
                    tails.pop(0)()
                for q4 in range(4):
                    ib = pc["pb"] % 3
                    pc["pb"] += 1
                    vb, vbk = ps[ib], "ps%d" % ib
                    for li in range(4):
                        ti = q4 * 4 + li
                        cl, m = divmod(ti, ntile)
                        t0 = cl + r * 128 * m
                        for kc in range(8):
                            P.op("tensor", lambda e, kc=kc, li=li, t0=t0: e.matmul(
                                vb[:, li * 128:(li + 1) * 128], lhsT=hT[:, kc, t0:t0 + r * 127 + 1:r], rhs=w[:, kc, 256:384],
                                start=(kc == 0), stop=(kc == 7)), reads=[wk, "hT"], writes=[vbk])
                    P.op("scalar", lambda e: e.copy(
                        out=dv[:, q4 * 4:(q4 + 1) * 4, :].rearrange("p n (b c) -> p n b c", b=3)[:, :, 0:3:2, :],
                        in_=vb[:].rearrange("p (n b c) -> p n b c", n=4, b=2)), reads=[vbk], writes=[kv])
                    yield

            def attn_gen(c, g, st_):
                r = DIL[g][1]
                n_sub = S // r
                ntile = n_sub // 128
                nq = ntile + 1
                cw = n_sub + 128
                dq, dk, dv = dqs[st_], dks[st_], dvs[st_]
                kq, kk, kv = "dq%d" % st_, "dk%d" % st_, "dv%d" % st_
                hb = [(ps[5], "ps5", accA, "accA"), (ps[6], "ps6", accB, "accB")]
                sbanks = (3, 4)
                pend = []

                def flush(keep):
                    while len(pend) > keep:
                        pend.pop(0)()

                if ntile == 1:
                    for cl0 in range(0, r, 4):
                        for hh in range(2):
                            hp = slice(hh * 64, (hh + 1) * 64)
                            isb = sbanks[pc["sb"] % 2]
                            pc["sb"] += 1
                            bs, bsk = ps[isb], "ps%d" % isb
                            for k4 in range(4):
                                cl = cl0 + k4
                                P.op("tensor", lambda e, k4=k4, cl=cl: e.matmul(
                                    bs[:, k4 * 128:(k4 + 1) * 128], lhsT=dk[hp, cl * 128:(cl + 1) * 128],
                                    rhs=dq[hp, cl * cw + 64:cl * cw + 192], start=True, stop=True), reads=[kq, kk], writes=[bsk])
                            ip = pc["px"] % 4
                            pc["px"] += 1
                            praw, prk = praws[ip], "praw_%d" % ip
                            pT, pTk = pTs[ip], "pT_%d" % ip
                            P.op("scalar", lambda e: e.activation(out=praw[:, :], in_=bs[:, :], func=AF.Exp, scale=0.125), reads=[bsk], writes=[prk])
                            P.op("vector", lambda e: e.tensor_tensor(
                                out=pT[:, :].rearrange("p (k i) -> p k i", k=4), in0=praw[:, :].rearrange("p (k i) -> p k i", k=4),
                                in1=Ets[st_][hh][:, 192:320].unsqueeze(1).to_broadcast([128, 4, 128]), op=ALU.mult),
                                reads=[prk, "Et%d_%d" % (st_, hh)], writes=[pTk])

                            def pv2(hh=hh, pT=pT, pTk=pTk, cl0=cl0):
                                bn, bnk, acc, akey = hb[hh]
                                vsl = slice(0, 128) if hh == 0 else slice(64, 192)
                                for k4 in range(4):
                                    P.op("tensor", lambda e, k4=k4: e.matmul(
                                        bn[:, k4 * 128:(k4 + 1) * 128], lhsT=dv[:, cl0 + k4, vsl], rhs=pT[:, k4 * 128:(k4 + 1) * 128],
                                        start=True, stop=True), reads=[kv, pTk], writes=[bnk])
                                accv = acc[:, :].rearrange("p (i c) -> p c i", c=r)[:, cl0:cl0 + 4, :]
                                P.op("vector", lambda e: e.tensor_tensor(out=accv, in0=accv, in1=bn[:, :].rearrange("p (k i) -> p k i", k=4),
                                                                         op=ALU.add), reads=[bnk, akey], writes=[akey])
                            pend.append(pv2)
                            flush(TUNE["lag"])
                            yield
                for cl in range(r if ntile > 1 else 0):
                    qbase = cl * cw
                    kbase = cl * n_sub
                    for j0 in range(0, nq, 4):
                        js = list(range(j0, min(j0 + 4, nq)))
                        npj = (len(js) + 1) // 2
                        for pi_, pj in enumerate(range(0, len(js), 2)):
                            pjs = js[pj:pj + 2]
                            for hh in range(2):
                                hp = slice(hh * 64, (hh + 1) * 64)
                                isb = sbanks[pc["sb"] % 2]
                                pc["sb"] += 1
                                bs, bsk = ps[isb], "ps%d" % isb
                                for qi, j in enumerate(pjs):
                                    qc = slice(qbase + 128 * j, qbase + 128 * j + 128)
                                    if j >= 1:
                                        kcs = slice(kbase + 128 * (j - 1), kbase + 128 * j)
                                        P.op("tensor", lambda e, qi=qi, qc=qc, kcs=kcs: e.matmul(
                                            bs[:, qi * 256:qi * 256 + 128], lhsT=dk[hp, kcs], rhs=dq[hp, qc], start=True, stop=True),
                                            reads=[kq, kk], writes=[bsk])
                                    if j < ntile:
                                        kcs = slice(kbase + 128 * j, kbase + 128 * (j + 1))
                                        P.op("tensor", lambda e, qi=qi, qc=qc, kcs=kcs: e.matmul(
                                            bs[:, qi * 256 + 128:qi * 256 + 256], lhsT=dk[hp, kcs], rhs=dq[hp, qc], start=True, stop=True),
                                            reads=[kq, kk], writes=[bsk])
                                lo = 128 if pjs[0] == 0 else 0
                                hi = len(pjs) * 256 - (128 if pjs[-1] == ntile else 0)
                                ip = pc["px"] % 4
                                pc["px"] += 1
                                praw, prk = praws[ip], "praw_%d" % ip
                                pT, pTk = pTs[ip], "pT_%d" % ip
                                P.op("scalar", lambda e, lo=lo, hi=hi: e.activation(out=praw[:, lo:hi], in_=bs[:, lo:hi], func=AF.Exp, scale=0.125),
                                     reads=[bsk], writes=[prk])
                                meng = "vector"
                                P.op(meng, lambda e, lo=lo, hi=hi: e.tensor_tensor(out=pT[:, lo:hi], in0=praw[:, lo:hi], in1=Ets[st_][hh][:, lo:hi],
                                                                                   op=ALU.mult),
                                     reads=[prk, "Et%d_%d" % (st_, hh)], writes=[pTk])
                                last_unit = (pi_ == npj - 1)

                                def pv(pjs=pjs, hh=hh, pT=pT, pTk=pTk, j0=j0, cl=cl, js=js, last_unit=last_unit):
                                    bn, bnk, acc, akey = hb[hh]
                                    vsl = slice(0, 128) if hh == 0 else slice(64, 192)
                                    for qi, j in enumerate(pjs):
                                        oc = slice((j - j0) * 128, (j - j0 + 1) * 128)
                                        tl = []
                                        if j >= 1:
                                            tl.append((j - 1, qi * 256))
                                        if j < ntile:
                                            tl.append((j, qi * 256 + 128))
                                        for ix, (m, pcx) in enumerate(tl):
                                            P.op("tensor", lambda e, oc=oc, m=m, pcx=pcx, ix=ix, n=len(tl): e.matmul(
                                                bn[:, oc], lhsT=dv[:, cl * ntile + m, vsl], rhs=pT[:, pcx:pcx + 128],
                                                start=(ix == 0), stop=(ix == n - 1)), reads=[kv, pTk], writes=[bnk])
                                    if last_unit:
                                        i_lo = max(128 * j0 - 64, 0)
                                        i_hi = min(128 * js[-1] + 64, n_sub)
                                        c_lo = i_lo - (128 * j0 - 64)
                                        ncol = i_hi - i_lo
                                        asl = slice(cl + r * i_lo, cl + r * (i_hi - 1) + 1, r)
                                        if g == 0:
                                            P.op("vector", lambda e: e.tensor_copy(out=acc[:, asl], in_=bn[:, c_lo:c_lo + ncol]),
                                                 reads=[bnk], writes=[akey])
                                        else:
                                            P.op("vector", lambda e: e.tensor_tensor(out=acc[:, asl], in0=acc[:, asl], in1=bn[:, c_lo:c_lo + ncol],
                                                                                     op=ALU.add), reads=[bnk, akey], writes=[akey])
                                pend.append(pv)
                                flush(TUNE["lag"])
                                yield
                flush(0)
                if g == 2:
                    for hf in range(2):
                        hsl = slice(hf * 1024, (hf + 1) * 1024)
                        P.op("scalar", lambda e: e.activation(out=tmpR[0:64, :], in_=accA[64:128, hsl], func=AF.Ln), reads=["accA"], writes=["tmpR"])
                        P.op("scalar", lambda e: e.activation(out=tmpR[64:128, :], in_=accB[0:64, hsl], func=AF.Ln), reads=["accB"], writes=["tmpR"])
                        P.op("scalar", lambda e: e.activation(out=tmpR[:, :], in_=tmpR[:, :], func=AF.Exp, scale=-1.0), reads=["tmpR"], writes=["tmpR"])
                        P.op("vector", lambda e: e.tensor_tensor(out=ydilT[0:64, c, hsl], in0=accA[0:64, hsl], in1=tmpR[0:64, :], op=ALU.mult),
                             reads=["accA", "tmpR"], writes=["Ydil"])
                        P.op("vector", lambda e: e.tensor_tensor(out=ydilT[64:128, c, hsl], in0=accB[64:128, hsl], in1=tmpR[64:128, :], op=ALU.mult),
                             reads=["accB", "tmpR"], writes=["Ydil"])

            def interleave(gens):
                gens = list(gens)
                while gens:
                    for gg in list(gens):
                        try:
                            next(gg)
                        except StopIteration:
                            gens.remove(gg)

            for st_ in range(2):
                P.op("vector", lambda e: e.memset(dvs[st_][:, :, 64:128], 1.0), writes=["dv%d" % st_])
            cgs = [(c, g) for c in range(4) for g in range(3)]

            def chain(*gs):
                for g_ in gs:
                    yield from g_

            def interleave_w(main, other, n_main, n_other):
                done_o = 0
                alive_o = other is not None
                i = 0
                while True:
                    try:
                        next(main)
                    except StopIteration:
                        break
                    i += 1
                    target = (i * n_other + max(n_main - 3, 1) - 1) // max(n_main - 3, 1)
                    while alive_o and done_o < target:
                        try:
                            next(other)
                            done_o += 1
                        except StopIteration:
                            alive_o = False
                while alive_o:
                    try:
                        next(other)
                    except StopIteration:
                        alive_o = False

            n_units = {0: 18, 1: 24, 2: 8}
            w_, wk_c = prep_begin(cgs[0][0], cgs[0][1], 0)
            interleave([prep_gen(cgs[0][0], cgs[0][1], 0, w_, wk_c)])
            for u, (c, g) in enumerate(cgs):
                other = None
                if u + 1 < len(cgs):
                    c2, g2 = cgs[u + 1]
                    w_, wk_c = prep_begin(c2, g2, (u + 1) % 2)
                    other = prep_gen(c2, g2, (u + 1) % 2, w_, wk_c)
                if TUNE["ilv"] == "even":
                    interleave_w(attn_gen(c, g, u % 2), other, n_units[g], 12)
                elif TUNE["ilv"] == "front":
                    interleave([attn_gen(c, g, u % 2)] + ([other] if other is not None else []))
                else:
                    if other is not None:
                        interleave([other])
                    interleave([attn_gen(c, g, u % 2)])
            dump("ydilT", ydilT, "Ydil", lambda d: d.rearrange("(c p) t -> p c t", p=128))
            if upto == "D":
                break
            P.barrier()

            wbr = []
            for bi, wsrc in enumerate((w_bd, w_br, w_bm)):
                sl_ = nxt("w", 3)
                wv_ = wbuf[sl_][:, 0:4096].rearrange("p (k n) -> p k n", k=4)
                P.op("gpsimd", lambda e: e.dma_start(out=wv_, in_=wsrc.rearrange("(k p) n -> p k n", p=128)),
                     writes=["wbuf%d" % sl_], dma="wbuf%d" % sl_)
                wbr.append((wv_, "wbuf%d" % sl_))
            ybs = ((ydilT, "Ydil"), (yretT, "Yret"), (ymemT, "Ymem"))
            s_keys = ["Et0_0", "Et0_1", "Et1_0", "Et1_1", "dq1", "dk1", "dv0", "dv1"] + ["praw_%d" % i_ for i_ in range(4)] + ["pT_%d" % i_ for i_ in range(4)]
            for oc in range(8):
                gsl = oc % 6
                wg = av(S0 + gsl * 3072, 3072).rearrange("p (k n) -> p k n", k=8)
                wgk = "gslot%d" % gsl
                for bi in range(3):
                    P.op("gpsimd", lambda e: e.dma_start(
                        out=wg[:, :, bi * 128:(bi + 1) * 128],
                        in_=rows(w_in[:, C_GATE + bi * 1024 + oc * 128:C_GATE + bi * 1024 + (oc + 1) * 128])),
                        writes=[wgk] + (s_keys if oc < 6 else []), dma=wgk)
                for tt in range(4):
                    tsl = slice(tt * 512, (tt + 1) * 512)
                    for bi in range(3):
                        bg, bgk = proj_fm(wg, wgk, bi * 128, tt)
                        sg = t32[1 + (bi % 2)]
                        sgk = "t32_%d" % (1 + (bi % 2))
                        P.op("scalar", lambda e, bg=bg, sg=sg: e.activation(out=sg[:], in_=bg[:, 0:512], func=AF.Sigmoid),
                             reads=[bgk], writes=[sgk])
                        bp, bpk = bank()
                        yb, ybk = ybs[bi]
                        for kc in range(4):
                            P.op("tensor", lambda e, kc=kc, bp=bp, bi=bi, oc=oc, yb=yb, tsl=tsl: e.matmul(
                                bp[:, 0:512], lhsT=wbr[bi][0][:, kc, oc * 128:(oc + 1) * 128], rhs=yb[:, kc, tsl],
                                start=(kc == 0), stop=(kc == 3)), reads=[wbr[bi][1], ybk], writes=[bpk])
                        if bi == 0:
                            P.op("vector", lambda e, bp=bp, sg=sg: e.tensor_tensor(out=t32[0][:], in0=sg[:], in1=bp[:, 0:512], op=ALU.mult),
                                 reads=[bpk, sgk], writes=["t32_0"])
                        else:
                            P.op("vector", lambda e, bp=bp, sg=sg: e.tensor_tensor(out=sg[:], in0=sg[:], in1=bp[:, 0:512], op=ALU.mult),
                                 reads=[bpk, sgk], writes=[sgk])
                            if bi == 1:
                                P.op("vector", lambda e, sg=sg: e.tensor_tensor(out=t32[0][:], in0=t32[0][:], in1=sg[:], op=ALU.add),
                                     reads=["t32_0", sgk], writes=["t32_0"])
                            else:
                                P.op("vector", lambda e, sg=sg, oc=oc, tsl=tsl: e.tensor_tensor(out=mergedT[:, oc, tsl], in0=t32[0][:], in1=sg[:],
                                                                                                op=ALU.add),
                                     reads=["t32_0", sgk], writes=["E"])
            dump("mergedT", mergedT, "E", lambda d: d.rearrange("(c p) t -> p c t", p=128))
            if upto == "G":
                break

            wo = [wload(rows(w_out[:, hf * 512:(hf + 1) * 512]), 512) for hf in range(2)]
            pend_o = None
            obufs = [(xt[0][:], "xt0", []), (xt[1][:], "xt1", []), (avf(0, 1024), "xa0", ["Yret"]), (avf(2048, 1024), "xa1", ["Yret"])]

            def oload(t):
                xo, xok, extra = obufs[t % 4]
                P.op("sync", lambda e: e.dma_start(out=xo, in_=x_d[b, t * 128:(t + 1) * 128, :]), writes=[xok] + extra, dma=xok)
            for t in range(3):
                oload(t)
            for t in range(16):
                if t + 3 < 16:
                    oload(t + 3)
                xo, xok, _ = obufs[t % 4]
                for hf in range(2):
                    bk, bkey = bank()
                    for kc in range(8):
                        P.op("tensor", lambda e, kc=kc: e.matmul(
                            bk[:, 0:512], lhsT=mergedT[:, kc, t * 128:(t + 1) * 128], rhs=wo[hf][0][:, kc, 0:512],
                            start=(kc == 0), stop=(kc == 7)), reads=["E", wo[hf][1]], writes=[bkey])
                    P.op("vector", lambda e: e.tensor_tensor(out=xo[:, hf * 512:(hf + 1) * 512], in0=xo[:, hf * 512:(hf + 1) * 512],
                                                             in1=bk[:, 0:512], op=ALU.add), reads=[bkey, xok], writes=[xok])
                P.op("sync", lambda e: e.dma_start(out=out_d[b, t * 128:(t + 1) * 128, :], in_=xo),
                     reads=[xok], writes=[("x2d", b, t, 0), ("x2d", b, t, 1)], dma=xok, is_out=True)
                xb_, xkeys = norm_a(xo, xok)
                if pend_o is not None:
                    norm_b(*pend_o)
                pend_o = (xb_, xkeys, hT, "hT", t * 128, P_N2G)
            norm_b(*pend_o)
            dump("h2T", hT[:], "hT", lambda d: d.rearrange("(k p) t -> p k t", p=128))
            if upto == "O":
                break
            P.barrier()

            P.op("vector", lambda e: e.memset(ubuf[:, 0:1], 0.0), writes=["ubuf"])
            P.op("vector", lambda e: e.memset(ubuf[:, 2049:2050], 0.0), writes=["ubuf"])
            for jp in range(11):
                src = [(rows(w_f1[:, sx * DFF + jp * 256:sx * DFF + (jp + 1) * 256]),
                        (lambda d, sx=sx: d[:, :, sx * 256:(sx + 1) * 256])) for sx in range(2)]
                w, wk = wload(src, 512)
                for jj in range(2):
                    j = 2 * jp + jj
                    for tt in range(4):
                        bu, buk = proj_fm(w, wk, jj * 128, tt)
                        P.op("scalar", lambda e, bu=bu, tt=tt: e.copy(out=ubuf[:, 1 + tt * 512:1 + (tt + 1) * 512], in_=bu[:, 0:512]),
                             reads=[buk], writes=["ubuf"])
                    cwc = lambda i, j=j: par[:, P_CW + i * NJ + j:P_CW + i * NJ + j + 1]
                    P.op("vector", lambda e, j=j, cwc=cwc: e.tensor_scalar(out=cbuf[:, :], in0=ubuf[:, 1:2049], scalar1=cwc(1),
                                                                           scalar2=par[:, P_CB + j:P_CB + j + 1], op0=ALU.mult, op1=ALU.add),
                         reads=["ubuf", "par"], writes=["cbuf"])
                    P.op("vector", lambda e, cwc=cwc: e.scalar_tensor_tensor(out=cbuf[:, :], in0=ubuf[:, 0:2048], scalar=cwc(0), in1=cbuf[:, :],
                                                                             op0=ALU.mult, op1=ALU.add), reads=["ubuf", "cbuf", "par"], writes=["cbuf"])
                    P.op("vector", lambda e, cwc=cwc: e.scalar_tensor_tensor(out=cbuf[:, :], in0=ubuf[:, 2:2050], scalar=cwc(2), in1=cbuf[:, :],
                                                                             op0=ALU.mult, op1=ALU.add), reads=["ubuf", "cbuf", "par"], writes=["cbuf"])
                    P.op("scalar", lambda e: e.activation(out=cbuf[:, :], in_=cbuf[:, :], func=AF.Gelu), reads=["cbuf"], writes=["cbuf"])
                    for tt in range(4):
                        bg, bgk = proj_fm(w, wk, 256 + jj * 128, tt)
                        P.op("vector", lambda e, bg=bg, tt=tt, j=j: e.tensor_tensor(out=yT[:, j, tt * 512:(tt + 1) * 512],
                                                                                    in0=cbuf[:, tt * 512:(tt + 1) * 512], in1=bg[:, 0:512], op=ALU.mult),
                             reads=[bgk, "cbuf"], writes=["yT"])
            dump("yT", yT, "yT", lambda d: d.rearrange("(j p) t -> p j t", p=128))
            if upto == "F":
                break
            def ffn_out_gen(b=b):
                tslots = [av(45056 + i_ * 4096, 4096).rearrange("p (k n) -> p k n", k=8) for i_ in range(3)]
                for hf in range(2):
                    chunks = []
                    for ci, (j0, j1) in enumerate(((0, 8), (8, 16), (16, 22))):
                        nk = j1 - j0
                        srcw = w_f2[j0 * 128:j1 * 128, hf * 512:(hf + 1) * 512].rearrange("(k p) n -> p k n", p=128)
                        if hf == 0:
                            w, wk = wload(srcw, 512, view=lambda d, nk=nk: d[:, 0:nk, :])
                        else:
                            w, wk = tslots[ci], "w2s%d" % ci
                            P.op("gpsimd", lambda e: e.dma_start(out=w[:, 0:nk, :], in_=srcw), writes=[wk, "ubuf", "cbuf"], dma=wk)
                        chunks.append((j0, j1, w, wk))
                    for tt in range(4):
                        bks = [bank() for _ in range(4)]
                        for (j0, j1, w, wk) in chunks:
                            for tl in range(4):
                                t = tt * 4 + tl
                                for j in range(j0, j1):
                                    P.op("tensor", lambda e, bk=bks[tl][0], j=j, t=t: e.matmul(
                                        bk[:, 0:512], lhsT=yT[:, j, t * 128:(t + 1) * 128], rhs=w[:, j - j0, 0:512],
                                        start=(j == 0), stop=(j == NJ - 1)), reads=["yT", wk], writes=[bks[tl][1]])
                        for tl in range(4):
                            t = tt * 4 + tl
                            it = nxt("t32", 3)
                            dsl = out_d[b, t * 128:(t + 1) * 128, hf * 512:(hf + 1) * 512]
                            P.op("sync", lambda e: e.dma_start(out=t32[it][:], in_=dsl),
                                 reads=[("x2d", b, t, hf)], writes=["t32_%d" % it], dma="t32_%d" % it)
                            P.op("vector", lambda e, bk=bks[tl][0]: e.tensor_tensor(out=t32[it][:], in0=t32[it][:], in1=bk[:, 0:512], op=ALU.add),
                                 reads=[bks[tl][1], "t32_%d" % it], writes=["t32_%d" % it])
                            P.op("sync", lambda e: e.dma_start(out=dsl, in_=t32[it][:]),
                                 reads=["t32_%d" % it], writes=[("x2d", b, t, hf)], dma="t32_%d" % it, is_out=True)
                        yield

            if b + 1 < nseq and TUNE["aov"]:
                interleave_w(phaseA_gen(b + 1, xa_small), ffn_out_gen(), 16, 8)
            else:
                for _ in ffn_out_gen():
                    pass
    P.emit()
    return nc, P


def _host_consts():
    c = np.zeros((128, NCF), np.float32)
    c[:, K_ID:K_ID + 128] = np.eye(128, dtype=np.float32)
    bd = np.zeros((128, 128), np.float32)
    bd[:64, :64] = 1.0 / 64
    bd[64:, 64:] = 1.0 / 64
    c[:, K_BD:K_BD + 128] = bd
    c[:, K_O128:K_O128 + 128] = 1.0 / 128
    c[:, K_ONE:K_ONE + 128] = 1.0
    kl = np.arange(128)[:, None].astype(np.float32)
    ql = np.arange(128)[None, :].astype(np.float32)
    BIG = 1.0e6
    A = np.where(kl >= ql, np.abs(kl - ql - 64), BIG)
    Bm = np.where(kl <= ql, np.abs(kl - ql + 64), BIG)
    c[:, K_DIST:K_DIST + 512] = np.concatenate([A, Bm, A, Bm], axis=1)
    j = kl
    i = ql
    c[:, K_DPOS:K_DPOS + 128] = np.maximum(i - j, 0)
    c[:, K_DNEG:K_DNEG + 128] = np.maximum(j - i, 0)
    c[:, K_MF:K_MF + 128] = (i >= j)
    c[:, K_MB:K_MB + 128] = (j > i)
    c[:, K_ZF] = 127 - np.arange(128)
    c[:, K_ZB] = np.arange(128)
    c[:, K_XF:K_XF + 128] = np.arange(128)[None, :] + 1
    c[:, K_XB:K_XB + 128] = 128 - np.arange(128)[None, :]
    return c


def _host_params(inp):
    p = np.zeros((128, NP), np.float32)
    p[:, P_N1G:P_N1G + 8] = inp["norm1_g"][0].reshape(8, 128).T
    p[:, P_MNG:P_MNG + 8] = inp["mem_norm_g"][0].reshape(8, 128).T
    p[:, P_N2G:P_N2G + 8] = inp["norm2_g"][0].reshape(8, 128).T
    p[:, P_DQG:P_DQG + 3] = np.tile(inp["dil_q_norm_g"][0].T, (2, 1))
    p[:, P_DKG:P_DKG + 3] = np.tile(inp["dil_k_norm_g"][0].T, (2, 1))
    p[:, P_MQG] = inp["mem_q_norm_g"][0]
    p[:, P_MKG] = inp["mem_k_norm_g"][0]
    p[:, P_GNG:P_GNG + 4] = inp["ret_gn_g"][0].reshape(4, 128).T
    p[:, P_CW:P_CW + 66] = inp["ffn_conv_w"][0].reshape(3, NJ, 128).transpose(2, 0, 1).reshape(128, 66)
    p[:, P_CB:P_CB + NJ] = inp["ffn_conv_b"][0].reshape(NJ, 128).T
    lgt = inp["ret_decay_logit"][0]
    p[:, P_LG:P_LG + 8] = np.tile(lgt.reshape(1, 8), (128, 1))
    for ch in range(2):
        for dr in range(2):
            p[:64, P_LGP + ch * 2 + dr] = lgt[dr, 2 * ch]
            p[64:, P_LGP + ch * 2 + dr] = lgt[dr, 2 * ch + 1]
    return p


_CACHE = {}


def _prep_maps(inputs, nseq, ncores):
    f = lambda a: np.ascontiguousarray(np.asarray(a, dtype=np.float32))
    shared = {
        "w_in": f(inputs["w_in"][0]), "w_mem_kv": f(inputs["w_mem_kv"][0]),
        "w_branch_dil": f(inputs["w_branch_dil"][0]), "w_branch_ret": f(inputs["w_branch_ret"][0]),
        "w_branch_mem": f(inputs["w_branch_mem"][0]), "w_out": f(inputs["w_out"][0]),
        "w_ffn_in": f(inputs["w_ffn_in"][0]), "w_ffn_out": f(inputs["w_ffn_out"][0]),
        "params": _host_params({k: np.asarray(v) for k, v in inputs.items()}), "consts": _host_consts(),
    }
    maps = []
    for c in range(ncores):
        m = dict(shared)
        m["x"] = f(inputs["x"][c * nseq:(c + 1) * nseq])
        m["mem"] = f(inputs["mem"][c * nseq:(c + 1) * nseq])
        maps.append(m)
    return maps


def kernel(**inputs):
    if "nc" not in _CACHE:
        _CACHE["nc"] = build(NSEQ)[0]
    nc = _CACHE["nc"]
    maps = _prep_maps(inputs, NSEQ, NCORES)
    res = run_bass_kernel_spmd(nc, maps, core_ids=list(range(NCORES)))
    return np.concatenate([r["out"] for r in res.results], axis=0).astype(np.float32)
```
